# Optimizing a Trainium2 kernel written in Bass

```python
import jax, jax.numpy as jnp
from jax import lax
import numpy as np


D_MODEL = 2048
BATCH = 16
SEQ = 2048
DEPTH = 2
DEC_BATCH = 4
DEC_SEQ = 4096
PAST_LEN = 128

N_META = 16
D_MIX = D_MODEL
D_CONV = D_MIX // 4
D_LRU = D_MIX // 2
D_POOL = D_MIX // 4
N_LRU_HEADS = 8
LRU_HEAD_DIM = D_LRU // N_LRU_HEADS
POOL_WINDOWS = (2, 4, 8, 16)
N_POOL_GROUPS = len(POOL_WINDOWS)
POOL_GROUP_DIM = D_POOL // N_POOL_GROUPS
SHORT_CONV_WIDTH = 3
LRU_CONV_WIDTH = 4
LRU_C = 8.0
D_FF = 5632
D_IN_PROJ = 3 * D_CONV + 2 * D_LRU + D_POOL
IN_PROJ_SPLITS = (D_CONV, 2 * D_CONV, 3 * D_CONV, 3 * D_CONV + D_LRU, 3 * D_CONV + 2 * D_LRU)
DEEPNORM_ALPHA = (2 * DEPTH) ** 0.25
DEEPNORM_BETA = (8 * DEPTH) ** -0.25
LN_EPS = 1e-5

kernel_name = 'hymba_parallel_conv_rglru_pool_encoder'


def _layernorm(x, g, b):
    xf = x.astype(jnp.float32)
    mu = jnp.mean(xf, axis=-1, keepdims=True)
    var = jnp.mean(jnp.square(xf - mu), axis=-1, keepdims=True)
    y = (xf - mu) * lax.rsqrt(var + LN_EPS) * g.astype(jnp.float32) + b.astype(jnp.float32)
    return y.astype(x.dtype)


def _post_norm(x, sub, g, b):
    return _layernorm(DEEPNORM_ALPHA * x + sub, g, b)


def _swiglu(x, w_gate, w_up, w_down):
    return (jax.nn.silu(x @ w_gate) * (x @ w_up)) @ w_down


def _short_conv_mixer(b_gate, c_gate, v, conv_w, conv_b):
    u = c_gate * v
    L = u.shape[1]
    pad = SHORT_CONV_WIDTH // 2
    up = jnp.pad(u, ((0, 0), (pad, pad), (0, 0)))
    y = conv_b + sum(up[:, k:k + L] * conv_w[k] for k in range(SHORT_CONV_WIDTH))
    return b_gate * y


def _causal_conv(u, w, b):
    K = w.shape[0]
    L = u.shape[1]
    up = jnp.pad(u, ((0, 0), (K - 1, 0), (0, 0)))
    return b + sum(up[:, k:k + L] * w[k] for k in range(K))


def _linear_combine(left, right):
    a1, b1 = left
    a2, b2 = right
    return a1 * a2, a2 * b1 + b2


def _rglru_direction(u, conv_w, conv_b, w_a, b_a, w_x, b_x, lam):
    xc = _causal_conv(u, conv_w, conv_b)
    Bsz, L, _ = xc.shape
    xh = xc.reshape(Bsz, L, N_LRU_HEADS, LRU_HEAD_DIM)
    r = jax.nn.sigmoid(jnp.einsum('blhi,hij->blhj', xh, w_a).reshape(Bsz, L, D_LRU) + b_a)
    i = jax.nn.sigmoid(jnp.einsum('blhi,hij->blhj', xh, w_x).reshape(Bsz, L, D_LRU) + b_x)
    log_a = (-LRU_C * jax.nn.softplus(-lam.astype(jnp.float32))) * r.astype(jnp.float32)
    a = jnp.exp(log_a)
    mult = jnp.sqrt(-jnp.expm1(2.0 * log_a))
    bterm = mult * (i * xc).astype(jnp.float32)
    _, h = lax.associative_scan(_linear_combine, (a, bterm), axis=1)
    return h.astype(u.dtype)


def _bidirectional_rglru(lru_x, lru_gate, conv_w, conv_b, w_a, b_a, w_x, b_x, lam):
    h_f = _rglru_direction(lru_x, conv_w[0], conv_b[0], w_a[0], b_a[0], w_x[0], b_x[0], lam[0])
    h_b = jnp.flip(_rglru_direction(jnp.flip(lru_x, axis=1), conv_w[1], conv_b[1], w_a[1], b_a[1],
                                    w_x[1], b_x[1], lam[1]), axis=1)
    return jax.nn.gelu(lru_gate) * (h_f + h_b)


def _pool_mixer(u, pool_w, pool_scale):
    Bsz, L, _ = u.shape
    uf = u.astype(jnp.float32)
    cs = jnp.concatenate([jnp.zeros((Bsz, 1, D_POOL), jnp.float32), jnp.cumsum(uf, axis=1)], axis=1)
    t = jnp.arange(L)
    outs = []
    for g, w in enumerate(POOL_WINDOWS):
        lo = jnp.maximum(t - w // 2, 0)
        hi = jnp.minimum(t + w // 2 - 1, L - 1)
        sl = slice(g * POOL_GROUP_DIM, (g + 1) * POOL_GROUP_DIM)
        csg = cs[..., sl]
        win_sum = jnp.take(csg, hi + 1, axis=1) - jnp.take(csg, lo, axis=1)
        count = (hi - lo + 1).astype(jnp.float32)[None, :, None]
        outs.append(win_sum / count - uf[..., sl])
    pooled = jnp.stack(outs, axis=2)
    mixed = jnp.einsum('blgi,gij->blgj', pooled, pool_w.astype(jnp.float32)).reshape(Bsz, L, D_POOL)
    return (mixed * pool_scale.astype(jnp.float32)).astype(u.dtype)


def _trunk(x, meta_tokens, ln_in_g, ln_in_b, ffn1_w_gate, ffn1_w_up, ffn1_w_down, ln1_g, ln1_b,
           w_in, conv_w, conv_b, lru_conv_w, lru_conv_b, lru_w_a, lru_b_a, lru_w_x, lru_b_x, lru_lambda,
           pool_w, pool_scale, w_out, ln2_g, ln2_b, ffn2_w_gate, ffn2_w_up, ffn2_w_down, ln3_g, ln3_b):
    Bsz = x.shape[0]
    meta = jnp.broadcast_to(meta_tokens.astype(x.dtype)[None], (Bsz, N_META, D_MODEL))
    h = _layernorm(jnp.concatenate([meta, x], axis=1), ln_in_g, ln_in_b)
    for l in range(DEPTH):
        h = _post_norm(h, 0.5 * _swiglu(h, ffn1_w_gate[l], ffn1_w_up[l], ffn1_w_down[l]), ln1_g[l], ln1_b[l])
        proj = h @ w_in[l]
        b_gate, c_gate, v, lru_x, lru_gate, pool_in = jnp.split(proj, IN_PROJ_SPLITS, axis=-1)
        y_conv = _short_conv_mixer(b_gate, c_gate, v, conv_w[l], conv_b[l])
        y_lru = _bidirectional_rglru(lru_x, lru_gate, lru_conv_w[l], lru_conv_b[l], lru_w_a[l], lru_b_a[l],
                                     lru_w_x[l], lru_b_x[l], lru_lambda[l])
        y_pool = _pool_mixer(pool_in, pool_w[l], pool_scale[l])
        mix = jnp.concatenate([y_conv, y_lru, y_pool], axis=-1) @ w_out[l]
        h = _post_norm(h, mix, ln2_g[l], ln2_b[l])
        h = _post_norm(h, 0.5 * _swiglu(h, ffn2_w_gate[l], ffn2_w_up[l], ffn2_w_down[l]), ln3_g[l], ln3_b[l])
    return h[:, N_META:]


def setup_inputs(seed: int = 0) -> dict:
    key = jax.random.key(seed)
    ks = jax.random.split(key, 32)
    f32 = jnp.float32

    def nrm(k, shape, scale):
        return jax.random.normal(k, shape, f32) * scale

    def gain(k, shape):
        return 1.0 + 0.02 * jax.random.normal(k, shape, f32)

    u = jax.random.uniform(ks[17], (DEPTH, 2, D_LRU), f32, minval=0.9, maxval=0.999)
    a_base = u ** (1.0 / LRU_C)
    lru_lambda = jnp.log(a_base) - jnp.log1p(-a_base)

    return {
        'x_prompt': nrm(ks[0], (BATCH, SEQ, D_MODEL), 1.0),
        'x_sample': nrm(ks[1], (DEC_BATCH, DEC_SEQ, D_MODEL), 1.0),
        'meta_tokens': nrm(ks[2], (N_META, D_MODEL), 1.0),
        'ln_in_g': gain(ks[3], (D_MODEL,)),
        'ln_in_b': nrm(ks[4], (D_MODEL,), 0.02),
        'ffn1_w_gate': nrm(ks[5], (DEPTH, D_MODEL, D_FF), D_MODEL ** -0.5),
        'ffn1_w_up': nrm(ks[6], (DEPTH, D_MODEL, D_FF), D_MODEL ** -0.5),
        'ffn1_w_down': nrm(ks[7], (DEPTH, D_FF, D_MODEL), DEEPNORM_BETA * D_FF ** -0.5),
        'ln1_g': gain(ks[8], (DEPTH, D_MODEL)),
        'ln1_b': nrm(ks[9], (DEPTH, D_MODEL), 0.02),
        'w_in': nrm(ks[10], (DEPTH, D_MODEL, D_IN_PROJ), D_MODEL ** -0.5),
        'conv_w': nrm(ks[11], (DEPTH, SHORT_CONV_WIDTH, D_CONV), SHORT_CONV_WIDTH ** -0.5),
        'conv_b': nrm(ks[12], (DEPTH, D_CONV), 0.02),
        'lru_conv_w': nrm(ks[13], (DEPTH, 2, LRU_CONV_WIDTH, D_LRU), LRU_CONV_WIDTH ** -0.5),
        'lru_conv_b': nrm(ks[14], (DEPTH, 2, D_LRU), 0.02),
        'lru_w_a': nrm(ks[15], (DEPTH, 2, N_LRU_HEADS, LRU_HEAD_DIM, LRU_HEAD_DIM), LRU_HEAD_DIM ** -0.5),
        'lru_b_a': nrm(ks[16], (DEPTH, 2, D_LRU), 0.02),
        'lru_w_x': nrm(ks[18], (DEPTH, 2, N_LRU_HEADS, LRU_HEAD_DIM, LRU_HEAD_DIM), LRU_HEAD_DIM ** -0.5),
        'lru_b_x': nrm(ks[19], (DEPTH, 2, D_LRU), 0.02),
        'lru_lambda': lru_lambda,
        'pool_w': nrm(ks[20], (DEPTH, N_POOL_GROUPS, POOL_GROUP_DIM, POOL_GROUP_DIM), POOL_GROUP_DIM ** -0.5),
        'pool_scale': gain(ks[21], (DEPTH, D_POOL)),
        'w_out': nrm(ks[22], (DEPTH, D_MIX, D_MODEL), DEEPNORM_BETA * D_MIX ** -0.5),
        'ln2_g': gain(ks[23], (DEPTH, D_MODEL)),
        'ln2_b': nrm(ks[24], (DEPTH, D_MODEL), 0.02),
        'ffn2_w_gate': nrm(ks[25], (DEPTH, D_MODEL, D_FF), D_MODEL ** -0.5),
        'ffn2_w_up': nrm(ks[26], (DEPTH, D_MODEL, D_FF), D_MODEL ** -0.5),
        'ffn2_w_down': nrm(ks[27], (DEPTH, D_FF, D_MODEL), DEEPNORM_BETA * D_FF ** -0.5),
        'ln3_g': gain(ks[28], (DEPTH, D_MODEL)),
        'ln3_b': nrm(ks[29], (DEPTH, D_MODEL), 0.02),
    }


def reference(x_prompt, x_sample, meta_tokens, ln_in_g, ln_in_b, ffn1_w_gate, ffn1_w_up, ffn1_w_down,
              ln1_g, ln1_b, w_in, conv_w, conv_b, lru_conv_w, lru_conv_b, lru_w_a, lru_b_a, lru_w_x, lru_b_x,
              lru_lambda, pool_w, pool_scale, w_out, ln2_g, ln2_b, ffn2_w_gate, ffn2_w_up, ffn2_w_down,
              ln3_g, ln3_b):
    weights = (meta_tokens, ln_in_g, ln_in_b, ffn1_w_gate, ffn1_w_up, ffn1_w_down, ln1_g, ln1_b,
               w_in, conv_w, conv_b, lru_conv_w, lru_conv_b, lru_w_a, lru_b_a, lru_w_x, lru_b_x, lru_lambda,
               pool_w, pool_scale, w_out, ln2_g, ln2_b, ffn2_w_gate, ffn2_w_up, ffn2_w_down, ln3_g, ln3_b)
    y_prompt = _trunk(x_prompt, *weights)
    y_sample = _trunk(x_sample, *weights)
    return (y_prompt, y_sample)
```

```python
import numpy as np
import concourse.bass as bass
import concourse.mybir as mybir
from concourse.bass_utils import run_bass_kernel_spmd

F32 = mybir.dt.float32
BF16 = mybir.dt.bfloat16
AF = mybir.ActivationFunctionType
ALU = mybir.AluOpType

H = 8
LN_EPS = 1e-5
LRU_C = 8.0


class Cfg:
    def __init__(self, D, F, FG, NCV, NH, N, TPS, NS, SEGT, DEPTH, LP, LS, NMETA=16):
        self.D, self.F, self.FG, self.NCV, self.NH = D, F, FG, NCV, NH
        self.N, self.TPS, self.NS, self.SEGT, self.DEPTH = N, TPS, NS, SEGT, DEPTH
        self.LP, self.LS, self.NMETA = LP, LS, NMETA
        self.KC = D // 128
        self.FC = F // 128
        self.NG = self.FC // FG
        self.CIN = 3 * NCV + 2 * NH + 4
        self.KM = NCV + NH + 4
        self.S = N * TPS
        self.T = self.S * NS
        self.SEG = N * SEGT
        assert self.T == 3 * self.SEG and self.FC % FG == 0
        assert self.SEG == LP + NMETA and 2 * self.SEG == LS + 2 * NMETA
        self.alpha = float((2 * DEPTH) ** 0.25)


FULL = Cfg(D=2048, F=5632, FG=11, NCV=4, NH=8, N=344, TPS=3, NS=6, SEGT=6, DEPTH=2, LP=2048, LS=4096)


def par_layout(c):
    off = {}
    pos = 0

    def add(name, w):
        nonlocal pos
        off[name] = (pos, w)
        pos += w
    add("lnin_g", c.KC); add("lnin_b", c.KC)
    for l in range(c.DEPTH):
        for k in (1, 2, 3):
            add(f"ln{k}_g{l}", c.KC); add(f"ln{k}_b{l}", c.KC)
        for j in range(c.NCV):
            add(f"cw{l}_{j}", 3); add(f"cb{l}_{j}", 1)
        for d in range(2):
            for h in range(c.NH):
                add(f"lw{l}_{d}_{h}", 4); add(f"lb{l}_{d}_{h}", 1)
                add(f"ba{l}_{d}_{h}", 1); add(f"bx{l}_{d}_{h}", 1); add(f"lam{l}_{d}_{h}", 1)
        for q in range(4):
            add(f"ps{l}_{q}", 1)
    return off, pos


def pack_params(c, inp):
    off, npar = par_layout(c)
    par = np.zeros((128, npar), np.float32)

    def put(name, arr):
        o, w = off[name]
        par[:, o:o + w] = arr

    def cols(v):
        v = np.asarray(v, np.float32)
        return v.reshape(-1, 128).T
    put("lnin_g", cols(inp["ln_in_g"])); put("lnin_b", cols(inp["ln_in_b"]))
    for l in range(c.DEPTH):
        for k in (1, 2, 3):
            put(f"ln{k}_g{l}", cols(inp[f"ln{k}_g"][l])); put(f"ln{k}_b{l}", cols(inp[f"ln{k}_b"][l]))
        for j in range(c.NCV):
            put(f"cw{l}_{j}", np.asarray(inp["conv_w"][l])[:, j * 128:(j + 1) * 128].T)
            put(f"cb{l}_{j}", np.asarray(inp["conv_b"][l])[j * 128:(j + 1) * 128, None])
        for d in range(2):
            for h in range(c.NH):
                sl = slice(h * 128, (h + 1) * 128)
                put(f"lw{l}_{d}_{h}", np.asarray(inp["lru_conv_w"][l, d])[:, sl].T)
                put(f"lb{l}_{d}_{h}", np.asarray(inp["lru_conv_b"][l, d])[sl, None])
                put(f"ba{l}_{d}_{h}", np.asarray(inp["lru_b_a"][l, d])[sl, None])
                put(f"bx{l}_{d}_{h}", np.asarray(inp["lru_b_x"][l, d])[sl, None])
                put(f"lam{l}_{d}_{h}", np.asarray(inp["lru_lambda"][l, d])[sl, None])
        for q in range(4):
            put(f"ps{l}_{q}", np.asarray(inp["pool_scale"][l])[q * 128:(q + 1) * 128, None])
    return par


class Buf:
    __slots__ = ("w", "r")

    def __init__(self):
        self.w = None
        self.r = []


class Eng:
    def __init__(self, nc, h, name, selfsync=True, has_sem=True):
        self.h = h
        self.sem = nc.semaphore(name).__enter__() if has_sem else None
        self.cnt = 0
        self.waited = {}
        self.selfsync = selfsync


class DSem:
    def __init__(self, nc, name):
        self.sem = nc.semaphore(name).__enter__()
        self.cnt = 0


class Prog:
    def __init__(self, nc):
        self.nc = nc
        self.P = Eng(nc, nc.tensor, "sP", selfsync=False)
        self.A = Eng(nc, nc.scalar, "sA")
        self.V = Eng(nc, nc.vector, "sV")
        self.G = Eng(nc, nc.gpsimd, "sG")
        self.Q = Eng(nc, nc.sync, "sQ", has_sem=False)
        self.engs = [self.P, self.A, self.V, self.G, self.Q]
        self.dsems = []
        self._nds = 0

    def dsem(self):
        self._nds += 1
        d = DSem(self.nc, f"d{self._nds}")
        self.dsems.append(d)
        return d

    def _deps(self, E, R, W):
        deps = {}
        for b in R:
            if b.w is not None:
                s, v = b.w
                if deps.get(s, 0) < v:
                    deps[s] = v
        for b in W:
            if b.w is not None:
                s, v = b.w
                if deps.get(s, 0) < v:
                    deps[s] = v
            for (s, v) in b.r:
                if deps.get(s, 0) < v:
                    deps[s] = v
        for s, v in deps.items():
            if s is E.sem and not E.selfsync:
                continue
            if E.waited.get(s, 0) < v:
                E.h.wait_ge(s, v)
                E.waited[s] = v

    def _record(self, tk, R, W):
        for b in R:
            b.r.append(tk)
            if len(b.r) > 64:
                m = {}
                for (s, v) in b.r:
                    if m.get(s, 0) < v:
                        m[s] = v
                b.r = list(m.items())
        for b in W:
            b.w = tk
            b.r = []

    def op(self, E, fn, R=(), W=(), inc=True):
        self._deps(E, R, W)
        ins = fn(E.h)
        if inc:
            E.cnt += 1
            ins.then_inc(E.sem, 1)
            tk = (E.sem, E.cnt)
        else:
            tk = (E.sem, E.cnt + 1)
        self._record(tk, R, W)
        return ins

    def dma(self, out, in_, R=(), W=(), ds=None, E=None, hold=None):
        E = E or self.Q
        self._deps(E, R, W)
        E.h.dma_start(out=out, in_=in_).then_inc(ds.sem, 16)
        ds.cnt += 16
        if hold is not None:
            hold.append((R, W))
        else:
            self._record((ds.sem, ds.cnt), R, W)

    def flush(self, hold, ds):
        for (R, W) in hold:
            self._record((ds.sem, ds.cnt), R, W)
        del hold[:]

    def barrier(self):
        cur = [(e.sem, e.cnt) for e in self.engs if e.sem is not None] + [(d.sem, d.cnt) for d in self.dsems]
        for E in self.engs:
            for s, v in cur:
                if v > 0 and s is not E.sem and E.waited.get(s, 0) < v:
                    E.h.wait_ge(s, v)
                    E.waited[s] = v


class Ring:
    def __init__(self, pg, tiles, loader, items):
        self.pg, self.tiles, self.loader, self.items = pg, tiles, loader, items
        self.R = len(tiles)
        self.bufs = [Buf() for _ in tiles]
        self.ds = [pg.dsem() for _ in tiles]
        self.next_load = 0
        self.next_use = 0

    def _load(self):
        i = self.next_load
        if i >= len(self.items):
            return
        k = i % self.R
        self.loader(self.tiles[k], self.items[i], self.bufs[k], self.ds[k])
        self.next_load += 1

    def prime(self):
        while self.next_load < min(self.R, len(self.items)) and self.next_load - self.next_use < self.R:
            self._load()

    def get(self, item):
        i = self.next_use
        assert self.items[i] == item, (self.items[i], item)
        k = i % self.R
        return self.tiles[k], self.bufs[k]

    def done(self):
        self.next_use += 1
        while self.next_load < len(self.items) and self.next_load - self.next_use < self.R:
            self._load()


def build_program(c, debug=False):
    nc = bass.Bass("TRN2", target_bir_lowering=False)
    KC, FC, FG, NG, N, S, T, SEG, TPS, NS = c.KC, c.FC, c.FG, c.NG, c.N, c.S, c.T, c.SEG, c.TPS, c.NS
    D, F, CIN, KM, NCV, NH, L = c.D, c.F, c.CIN, c.KM, c.NCV, c.NH, c.DEPTH
    DIN = CIN * 128
    DMX = KM * 128
    WB = SEG + 2 * H
    off, NPAR = par_layout(c)

    def din(name, shape, dt=F32):
        return nc.dram_tensor(name, shape, dt, kind="ExternalInput").ap()
    xin = din("xin", [T, D])
    par_d = din("par", [128, NPAR])
    flags_d = din("flags", [128, 4])
    invc_d = din("invc", [4, T])
    ident_d = din("ident", [128, 128])
    w_g = [din("ffn1_w_gate", [L, D, F]), din("ffn2_w_gate", [L, D, F])]
    w_u = [din("ffn1_w_up", [L, D, F]), din("ffn2_w_up", [L, D, F])]
    w_d = [din("ffn1_w_down", [L, F, D]), din("ffn2_w_down", [L, F, D])]
    w_in = din("w_in", [L, D, DIN])
    w_out = din("w_out", [L, DMX, D])
    lwa = din("lru_w_a", [L, 2, NH, 128, 128])
    lwx = din("lru_w_x", [L, 2, NH, 128, 128])
    plw = din("pool_w", [L, 4, 128, 128])
    yout = nc.dram_tensor("yout", [T, D], F32, kind="ExternalOutput").ap()

    def dscr(name, shape, dt):
        if debug and name in ("s_hres0", "s_proj", "s_mix"):
            return nc.dram_tensor(name, shape, dt, kind="ExternalOutput").ap()
        return nc.dram_tensor(name, shape, dt).ap()
    s_gu = [[dscr(f"s_gu{l}_{k}", [FC, 128, 2, KC * 128], BF16) for k in range(2)] for l in range(L)]
    s_d = [[dscr(f"s_d{l}_{k}", [NG, KC, 128, FG * 128], BF16) for k in range(2)] for l in range(L)]
    s_in = [dscr(f"s_in{l}", [CIN, 128, KC * 128], BF16) for l in range(L)]
    s_out = [dscr(f"s_out{l}", [KC, 128, KM * 128], BF16) for l in range(L)]
    s_hres = [dscr(f"s_hres{l}", [KC, 128, T], F32) for l in range(L)]
    s_proj = dscr("s_proj", [CIN, 128, T], F32)
    s_mix = dscr("s_mix", [KM, 128, T], BF16)

    pg = Prog(nc)
    P, A, V, G, Q = pg.P, pg.A, pg.V, pg.G, pg.Q
    op, dma = pg.op, pg.dma

    from contextlib import ExitStack
    stack_holder = [ExitStack()]

    uniq = [0]

    def sb(name, shape, dt):
        uniq[0] += 1
        return stack_holder[0].enter_context(nc.sbuf_tensor(f"{name}_{uniq[0]}", shape, dt))

    glob_stack = stack_holder[0]
    par = sb("par_sb", [128, NPAR], F32); par_b = Buf()
    flg = sb("flg", [128, 4], F32)
    ident = sb("ident_sb", [128, 128], F32)
    onesb = sb("onesb", [128, 128], BF16)
    coef = sb("coef", [128, L * 2 * NH * 2], F32); coef_b = Buf()
    ctmp = sb("ctmp", [128, L * 2 * NH], F32)
    WG = max(KC, KM) * 128
    gu_t = [sb(f"gu{i}", [128, 2, KC * 128], BF16) for i in range(2)]
    d_t = [sb(f"dd{i}", [128, FG * 128], BF16) for i in range(3)]
    io_t = [sb(f"io{i}", [128, WG], BF16) for i in range(2)]
    psum = nc.psum_tensor("psum", [128, 8, 512], F32).__enter__()
    pb = [Buf() for _ in range(8)]
    cst_b = Buf()
    ds0 = pg.dsem()
    dma(par[:], par_d, W=[par_b], ds=ds0)
    dma(flg[:], flags_d, W=[cst_b], ds=ds0)
    dma(ident[:], ident_d, W=[cst_b], ds=ds0)
    op(G, lambda e: e.memset(onesb[:], 1.0 / D), W=[cst_b])
    cbias = sb("cbias", [128, 4], F32)
    op(G, lambda e: e.memset(cbias[:, 0:1], LN_EPS), W=[cst_b])
    op(G, lambda e: e.memset(cbias[:, 1:2], 1.0), W=[cst_b])
    op(G, lambda e: e.memset(cbias[:, 2:3], 0.0), W=[cst_b])

    def pc(name, j=0, w=1):
        o, _ = off[name]
        return par[:, o + j:o + j + w]
    par_al = sb("par_al", [128, NPAR], F32)
    paral_b = Buf()
    op(V, lambda e: e.tensor_scalar(out=par_al[:], in0=par[:], scalar1=c.alpha, scalar2=None, op0=ALU.mult), R=[par_b], W=[paral_b])

    def pca(name, j=0, w=1):
        o, _ = off[name]
        return par_al[:, o + j:o + j + w]

    for l in range(L):
        for d in range(2):
            for h in range(NH):
                i = (l * 2 + d) * NH + h
                op(A, lambda e: e.activation(out=ctmp[:, i:i + 1], in_=pc(f"lam{l}_{d}_{h}"), func=AF.Exp, scale=-1.0),
                   R=[par_b], W=[coef_b])
    op(A, lambda e: e.activation(out=ctmp[:], in_=ctmp[:], func=AF.Ln, bias=cbias[:, 1:2], scale=1.0), R=[coef_b, cst_b], W=[coef_b])
    cview = coef[:].rearrange("p (i two) -> p i two", two=2)
    op(V, lambda e: e.tensor_scalar(out=cview[:, :, 0], in0=ctmp[:], scalar1=-LRU_C, scalar2=None, op0=ALU.mult),
       R=[coef_b], W=[coef_b])
    op(V, lambda e: e.tensor_scalar(out=cview[:, :, 1], in0=ctmp[:], scalar1=-2.0 * LRU_C, scalar2=None, op0=ALU.mult),
       R=[coef_b], W=[coef_b])

    CE = 512
    cis = [sb(f"cv_in{i}", [128, CE], F32) for i in range(3)]
    NCO = 4
    cos = [sb(f"cv_o{i}", [128, CE], BF16) for i in range(NCO)]
    cib, cob = [Buf() for _ in range(3)], [Buf() for _ in range(NCO)]
    cid, cod = [pg.dsem() for _ in range(3)], [pg.dsem() for _ in range(NCO)]
    phase_order = []
    for l in range(L):
        for k in range(2):
            if k == 1:
                phase_order.append(("out", l))
            for g in range(NG):
                phase_order.append(("gu", l, k, g))
                phase_order.append(("d", l, k, g))
            if k == 0:
                phase_order.append(("in", l))
    phase_idx = {p: i for i, p in enumerate(phase_order)}
    phase_bufs = {p: [Buf() for _ in range(NCO)] for p in phase_order}
    csteps = []

    def add_steps(ph, src_fn, dst_fn, a):
        m = CE // 128
        a0 = 0
        while a0 < a:
            an = min(m, a - a0)
            csteps.append((ph, src_fn(a0, an), dst_fn(a0, an), an))
            a0 += an
    for ph in phase_order:
        if ph[0] == "gu":
            _, l, k, g = ph
            for f in range(g * FG, (g + 1) * FG):
                for wi, wsrc in enumerate((w_g, w_u)):
                    add_steps(ph, lambda a0, an, wsrc=wsrc, f=f, l=l, k=k: wsrc[k][l, a0 * 128:(a0 + an) * 128, f * 128:(f + 1) * 128]
                              .rearrange("(kc p) j -> p kc j", p=128),
                              lambda a0, an, f=f, l=l, k=k, wi=wi: s_gu[l][k][f, :, wi, a0 * 128:(a0 + an) * 128], KC)
        elif ph[0] == "d":
            _, l, k, g = ph
            for dch in range(KC):
                add_steps(ph, lambda a0, an, l=l, k=k, g=g, dch=dch: w_d[k][l, (g * FG + a0) * 128:(g * FG + a0 + an) * 128, dch * 128:(dch + 1) * 128]
                          .rearrange("(fc p) j -> p fc j", p=128),
                          lambda a0, an, l=l, k=k, g=g, dch=dch: s_d[l][k][g, dch, :, a0 * 128:(a0 + an) * 128], FG)
        elif ph[0] == "in":
            _, l = ph
            for ci in range(CIN):
                add_steps(ph, lambda a0, an, l=l, ci=ci: w_in[l, a0 * 128:(a0 + an) * 128, ci * 128:(ci + 1) * 128].rearrange("(kc p) j -> p kc j", p=128),
                          lambda a0, an, l=l, ci=ci: s_in[l][ci, :, a0 * 128:(a0 + an) * 128], KC)
        else:
            _, l = ph
            for dch in range(KC):
                add_steps(ph, lambda a0, an, l=l, dch=dch: w_out[l, a0 * 128:(a0 + an) * 128, dch * 128:(dch + 1) * 128].rearrange("(kc p) j -> p kc j", p=128),
                          lambda a0, an, l=l, dch=dch: s_out[l][dch, :, a0 * 128:(a0 + an) * 128], KM)

    class BgConv:
        LOOK = 2
        LB = 2

        def __init__(self):
            self.i = 0
            self.in_issued = 0
            self.out_issued = 0
            self.first_ffn_end = phase_idx[("d", 0, 0, NG - 1)]

        def _issue_in(self, j):
            ph, src3, dst2, an = csteps[j]
            k = j % 3
            dma(cis[k][:, 0:an * 128].rearrange("p (a b) -> p a b", a=an), src3, W=[cib[k]], ds=cid[k])

        def _issue_out(self, j):
            ph, src3, dst2, an = csteps[j]
            k = j % NCO
            dma(dst2, cos[k][:, 0:an * 128], R=[cob[k]], W=[phase_bufs[ph][k]], ds=cod[k])

        def step(self):
            i = self.i
            if i >= len(csteps):
                return False
            while self.in_issued < min(i + 1 + self.LOOK, len(csteps)):
                self._issue_in(self.in_issued)
                self.in_issued += 1
            while self.out_issued < i - self.LB + 1:
                self._issue_out(self.out_issued)
                self.out_issued += 1
            ph, src3, dst2, an = csteps[i]
            n = an * 128
            ki, ko = i % 3, i % NCO
            op(G, lambda e: e.tensor_copy(out=cos[ko][:, 0:n], in_=cis[ki][:, 0:n]), R=[cib[ki]], W=[cob[ko]])
            self.i += 1
            return True

        def flush_out(self):
            while self.out_issued < self.i:
                self._issue_out(self.out_issued)
                self.out_issued += 1

        def steps(self, n):
            for _ in range(n):
                if not self.step():
                    break
            if self.i >= len(csteps):
                self.flush_out()

        def slot(self, kind):
            if self.i >= len(csteps):
                return
            early = phase_idx[csteps[self.i][0]] <= self.first_ffn_end
            sub = (KC + 3) // 4
            if kind == "gu":
                self.steps(2 * sub if early else sub)
            elif kind == "d":
                self.steps((FG + 3) // 4 if early else max(1, (FG + 3) // 6))
            else:
                self.steps(sub if early else max(1, sub // 2))

        def ensure(self, ph):
            pi = phase_idx[ph]
            while self.i < len(csteps) and phase_idx[csteps[self.i][0]] <= pi:
                self.step()
            self.flush_out()
    bg = BgConv()

    gu_items, d_items, io_items = [], [], []

    def plan_ffn(l, k):
        for g in range(NG):
            for f in range(g * FG, (g + 1) * FG):
                gu_items.append((l, k, f))
            for dch in range(KC):
                d_items.append((l, k, g, dch))

    def plan_A(l):
        plan_ffn(l, 0)
        for ci in range(CIN):
            io_items.append(("in", l, ci))

    def plan_C(l):
        for dch in range(KC):
            io_items.append(("out", l, dch))
        plan_ffn(l, 1)
    for s in range(NS):
        plan_A(0)
    for l in range(L):
        for s in range(NS):
            plan_C(l)
            if l + 1 < L:
                plan_A(l + 1)

    def load_gu(tile, it, b, ds):
        l, k, f = it
        ph = ("gu", l, k, f // FG)
        bg.ensure(ph)
        dma(tile[:], s_gu[l][k][f], R=phase_bufs[ph], W=[b], ds=ds)

    def load_d(tile, it, b, ds):
        l, k, g, dch = it
        ph = ("d", l, k, g)
        bg.ensure(ph)
        dma(tile[:], s_d[l][k][g, dch], R=phase_bufs[ph], W=[b], ds=ds)

    def load_io(tile, it, b, ds):
        kind, l, ci = it
        ph = (kind, l)
        bg.ensure(ph)
        if kind == "in":
            dma(tile[:, 0:KC * 128], s_in[l][ci], R=phase_bufs[ph], W=[b], ds=ds)
        else:
            dma(tile[:, 0:KM * 128], s_out[l][ci], R=phase_bufs[ph], W=[b], ds=ds)
    r_gu = Ring(pg, gu_t, load_gu, gu_items)
    r_d = Ring(pg, d_t, load_d, d_items)
    r_io = Ring(pg, io_t, load_io, io_items)
    r_gu.prime(); r_d.prime()

    pA_rot = [0]
    pB_rot = [0]

    def bankB():
        k = 4 + (pB_rot[0] % 2)
        pB_rot[0] += 1
        return k

    def token_passes(first, l_c, l_a, last):
        stack_holder[0] = ExitStack()
        res = sb("res", [128, KC, S], F32)
        hb = sb("hb", [128, max(KC, KM), S], BF16)
        hT = sb("hT", [128, FG, S], BF16)
        xt = sb("xt", [128, D], F32)
        ybt = [sb(f"ybt{i}", [128, min(4, KC), N], BF16) for i in range(2)]
        yst = [sb(f"yst{i}", [128, min(4, KC), N], BF16) for i in range(2)]
        means = [sb(f"mean{i}", [128, N], F32) for i in range(TPS)]
        rstds = [sb(f"rstd{i}", [128, N], F32) for i in range(TPS)]
        mean_bs = [Buf() for _ in range(TPS)]; rstd_bs = [Buf() for _ in range(TPS)]
        sg = [sb(f"sg{i}", [128, N], F32) for i in range(2)]
        stg = [sb(f"stg{i}", [128, N], F32) for i in range(3)]
        bst = sb("bst", [128, 4 * 6], F32); bag = sb("bag", [128, 2], F32); brs = sb("brs", [128, 1], F32)
        res_b = [[Buf() for _ in range(TPS)] for _ in range(KC)]
        hb_b = [[Buf() for _ in range(TPS)] for _ in range(max(KC, KM))]
        hT_b = [[Buf() for _ in range(TPS)] for _ in range(FG)]
        xt_b = Buf(); ybt_b = [Buf(), Buf()]; yst_b = [Buf(), Buf()]
        sg_b = [Buf(), Buf()]; stg_b = [Buf() for _ in range(3)]
        stg_ds = [pg.dsem() for _ in range(3)]
        bn_b = Buf()
        xt_ds = pg.dsem(); act_ds = pg.dsem(); act2_ds = pg.dsem(); hres_ds = pg.dsem(); out_ds = pg.dsem()
        cnts = {"ln": 0, "stg": 0}

        def ts(n):
            return slice(n * N, (n + 1) * N)

        KG = min(4, KC)

        def layer_norm(gname, bname, scaled=True, need_hb=True):
            pcs = pca if scaled else pc
            assert TPS <= 3
            for n in range(TPS):
                pm, pe2 = 2 * n, 2 * n + 1
                for gk in range(KC // KG):
                    ks = slice(gk * KG, (gk + 1) * KG)
                    rb = [res_b[kc][n] for kc in range(gk * KG, (gk + 1) * KG)]
                    j = cnts["ln"] % 2
                    cnts["ln"] += 1
                    op(V, lambda e: e.tensor_copy(out=ybt[j][:], in_=res[:, ks, ts(n)]), R=rb, W=[ybt_b[j]])
                    op(A, lambda e: e.activation(out=yst[j][:], in_=res[:, ks, ts(n)], func=AF.Square), R=rb, W=[yst_b[j]])
                    for q in range(KG):
                        kc = gk * KG + q
                        op(P, lambda e: e.matmul(psum[:, pm, 0:N], lhsT=onesb[:], rhs=ybt[j][:, q, :], start=(kc == 0), stop=(kc == KC - 1)),
                           R=[ybt_b[j], cst_b], W=[pb[pm]])
                        op(P, lambda e: e.matmul(psum[:, pe2, 0:N], lhsT=onesb[:], rhs=yst[j][:, q, :], start=(kc == 0), stop=(kc == KC - 1)),
                           R=[yst_b[j], cst_b], W=[pb[pe2]])
            for n in range(TPS):
                pm, pe2 = 2 * n, 2 * n + 1
                mean, rstd, mean_b, rstd_b = means[n], rstds[n], mean_bs[n], rstd_bs[n]
                op(V, lambda e: e.tensor_copy(out=mean[:], in_=psum[:, pm, 0:N]), R=[pb[pm]], W=[mean_b])
                op(V, lambda e: e.tensor_tensor(out=rstd[:], in0=mean[:], in1=mean[:], op=ALU.mult), R=[mean_b], W=[rstd_b])
                op(V, lambda e: e.tensor_tensor(out=rstd[:], in0=psum[:, pe2, 0:N], in1=rstd[:], op=ALU.subtract), R=[pb[pe2], rstd_b], W=[rstd_b])
                op(A, lambda e: e.activation(out=rstd[:], in_=rstd[:], func=AF.Sqrt, bias=cbias[:, 0:1], scale=1.0), R=[rstd_b, cst_b], W=[rstd_b])
                op(V, lambda e: e.reciprocal(out=rstd[:], in_=rstd[:]), R=[rstd_b], W=[rstd_b])
            for n in range(TPS):
                mean, rstd, mean_b, rstd_b = means[n], rstds[n], mean_bs[n], rstd_bs[n]
                mean_bc = mean[:].unsqueeze(1).to_broadcast([128, KG, N])
                rstd_bc = rstd[:].unsqueeze(1).to_broadcast([128, KG, N])
                for gk in range(KC // KG):
                    ks = slice(gk * KG, (gk + 1) * KG)
                    rb = [res_b[kc][n] for kc in range(gk * KG, (gk + 1) * KG)]
                    op(V, lambda e: e.tensor_tensor(out=res[:, ks, ts(n)], in0=res[:, ks, ts(n)], in1=mean_bc, op=ALU.subtract),
                       R=rb + [mean_b], W=rb)
                    op(V, lambda e: e.tensor_tensor(out=res[:, ks, ts(n)], in0=res[:, ks, ts(n)], in1=rstd_bc, op=ALU.mult),
                       R=rb + [rstd_b], W=rb)
                    for kc in range(gk * KG, (gk + 1) * KG):
                        if need_hb:
                            op(A, lambda e: e.activation(out=hb[:, kc, ts(n)], in_=res[:, kc, ts(n)], func=AF.Identity, scale=pc(gname, kc),
                                                         bias=pc(bname, kc)), R=[res_b[kc][n], par_b], W=[hb_b[kc][n]])
                        if kc % 2 == 0:
                            op(V, lambda e: e.tensor_scalar(out=res[:, kc, ts(n)], in0=res[:, kc, ts(n)], scalar1=pcs(gname, kc), scalar2=pcs(bname, kc),
                                                            op0=ALU.mult, op1=ALU.add), R=[res_b[kc][n], par_b, paral_b], W=[res_b[kc][n]])
                        else:
                            op(A, lambda e: e.activation(out=res[:, kc, ts(n)], in_=res[:, kc, ts(n)], func=AF.Identity, scale=pcs(gname, kc),
                                                         bias=pcs(bname, kc)), R=[res_b[kc][n], par_b, paral_b], W=[res_b[kc][n]])

        def ffn(l, k):
            for g in range(NG):
                for f in range(g * FG, (g + 1) * FG):
                    fl = f - g * FG
                    wt, wb = r_gu.get((l, k, f))
                    for n in range(TPS):
                        p0 = 2 * (pA_rot[0] % 2)
                        pA_rot[0] += 1
                        for kc in range(KC):
                            op(P, lambda e: e.matmul(psum[:, p0, 0:N], lhsT=wt[:, 0, kc * 128:(kc + 1) * 128], rhs=hb[:, kc, ts(n)],
                                                     start=(kc == 0), stop=(kc == KC - 1)), R=[wb, hb_b[kc][n]], W=[pb[p0]], inc=(kc == KC - 1))
                        for kc in range(KC):
                            op(P, lambda e: e.matmul(psum[:, p0 + 1, 0:N], lhsT=wt[:, 1, kc * 128:(kc + 1) * 128], rhs=hb[:, kc, ts(n)],
                                                     start=(kc == 0), stop=(kc == KC - 1)), R=[wb, hb_b[kc][n]], W=[pb[p0 + 1]], inc=(kc == KC - 1))
                        j = (p0 // 2)
                        op(A, lambda e: e.activation(out=sg[j][:], in_=psum[:, p0, 0:N], func=AF.Silu), R=[pb[p0]], W=[sg_b[j]])
                        op(V, lambda e: e.tensor_tensor(out=hT[:, fl, ts(n)], in0=psum[:, p0 + 1, 0:N], in1=sg[j][:], op=ALU.mult),
                           R=[pb[p0 + 1], sg_b[j]], W=[hT_b[fl][n]])
                    r_gu.done()
                    bg.slot("gu")
                for dch in range(KC):
                    wt, wb = r_d.get((l, k, g, dch))
                    for n in range(TPS):
                        bk = bankB()
                        for fl in range(FG):
                            op(P, lambda e: e.matmul(psum[:, bk, 0:N], lhsT=wt[:, fl * 128:(fl + 1) * 128], rhs=hT[:, fl, ts(n)],
                                                     start=(fl == 0), stop=(fl == FG - 1)), R=[wb, hT_b[fl][n]], W=[pb[bk]], inc=(fl == FG - 1))
                        op(V, lambda e: e.scalar_tensor_tensor(out=res[:, dch, ts(n)], in0=psum[:, bk, 0:N], scalar=0.5, in1=res[:, dch, ts(n)],
                                                               op0=ALU.mult, op1=ALU.add), R=[pb[bk], res_b[dch][n]], W=[res_b[dch][n]])
                    r_d.done()
                    bg.slot("d")

        for s in range(NS):
            t0 = s * S
            if first:
                nblk = (S + 127) // 128
                for b in range(nblk):
                    nb = min(128, S - b * 128)
                    c0 = b * 128
                    dma(xt[0:nb, :], xin[t0 + c0:t0 + c0 + nb, :], W=[xt_b], ds=xt_ds)
                    nch = (D + 511) // 512
                    for q in range(nch):
                        w = min(512, D - q * 512)
                        op(V, lambda e: e.bn_stats(out=bst[0:nb, q * 6:(q + 1) * 6], in_=xt[0:nb, q * 512:q * 512 + w]), R=[xt_b], W=[bn_b])
                    op(V, lambda e: e.bn_aggr(out=bag[0:nb, :], in_=bst[0:nb, 0:nch * 6]), R=[bn_b], W=[bn_b])
                    op(A, lambda e: e.activation(out=brs[0:nb, :], in_=bag[0:nb, 1:2], func=AF.Sqrt, bias=cbias[0:nb, 0:1], scale=1.0),
                       R=[bn_b, cst_b], W=[bn_b])
                    op(V, lambda e: e.reciprocal(out=brs[0:nb, :], in_=brs[0:nb, :]), R=[bn_b], W=[bn_b])
                    op(V, lambda e: e.tensor_scalar(out=xt[0:nb, :], in0=xt[0:nb, :], scalar1=bag[0:nb, 0:1], scalar2=brs[0:nb, 0:1],
                                                    op0=ALU.subtract, op1=ALU.mult), R=[xt_b, bn_b], W=[xt_b])
                    for kc in range(KC):
                        bk = bankB()
                        op(P, lambda e: e.transpose(psum[:, bk, 0:nb], xt[0:nb, kc * 128:(kc + 1) * 128], ident[0:nb, 0:nb]),
                           R=[xt_b, cst_b], W=[pb[bk]])
                        tb = [res_b[kc][n] for n in range(TPS) if n * N < c0 + nb and (n + 1) * N > c0]
                        op(V, lambda e: e.tensor_scalar(out=res[:, kc, c0:c0 + nb], in0=psum[:, bk, 0:nb], scalar1=pca("lnin_g", kc),
                                                        scalar2=pca("lnin_b", kc), op0=ALU.mult, op1=ALU.add), R=[pb[bk], par_b, paral_b], W=tb)
                        tbh = [hb_b[kc][n] for n in range(TPS) if n * N < c0 + nb and (n + 1) * N > c0]
                        op(A, lambda e: e.activation(out=hb[:, kc, c0:c0 + nb], in_=res[:, kc, c0:c0 + nb], func=AF.Copy, scale=1.0 / c.alpha),
                           R=tb, W=tbh)
            if l_c is not None:
                l = l_c
                grp = []
                for kc in range(KM):
                    dma(hb[:, kc, :], s_mix[kc, :, t0:t0 + S], W=hb_b[kc], ds=act_ds, hold=grp)
                pg.flush(grp, act_ds)
                for kc in range(KC):
                    dma(res[:, kc, :], s_hres[l][kc, :, t0:t0 + S], W=res_b[kc], ds=act2_ds, hold=grp)
                pg.flush(grp, act2_ds)
                for dch in range(KC):
                    wt, wb = r_io.get(("out", l, dch))
                    for n in range(TPS):
                        bk = bankB()
                        for kc in range(KM):
                            op(P, lambda e: e.matmul(psum[:, bk, 0:N], lhsT=wt[:, kc * 128:(kc + 1) * 128], rhs=hb[:, kc, ts(n)],
                                                     start=(kc == 0), stop=(kc == KM - 1)), R=[wb, hb_b[kc][n]], W=[pb[bk]], inc=(kc == KM - 1))
                        op(V, lambda e: e.tensor_tensor(out=res[:, dch, ts(n)], in0=psum[:, bk, 0:N], in1=res[:, dch, ts(n)], op=ALU.add),
                           R=[pb[bk], res_b[dch][n]], W=[res_b[dch][n]])
                    r_io.done()
                    bg.slot("io")
                layer_norm(f"ln2_g{l}", f"ln2_b{l}")
                ffn(l, 1)
                layer_norm(f"ln3_g{l}", f"ln3_b{l}", scaled=not last, need_hb=not last)
            if l_a is not None:
                l = l_a
                ffn(l, 0)
                r_io.prime()
                layer_norm(f"ln1_g{l}", f"ln1_b{l}")
                grp = []
                for kc in range(KC):
                    dma(s_hres[l][kc, :, t0:t0 + S], res[:, kc, :], R=res_b[kc], ds=hres_ds, hold=grp)
                pg.flush(grp, hres_ds)
                for ci in range(CIN):
                    wt, wb = r_io.get(("in", l, ci))
                    for n in range(TPS):
                        bk = bankB()
                        for kc in range(KC):
                            op(P, lambda e: e.matmul(psum[:, bk, 0:N], lhsT=wt[:, kc * 128:(kc + 1) * 128], rhs=hb[:, kc, ts(n)],
                                                     start=(kc == 0), stop=(kc == KC - 1)), R=[wb, hb_b[kc][n]], W=[pb[bk]], inc=(kc == KC - 1))
                        j = cnts["stg"] % 3
                        cnts["stg"] += 1
                        if j == 1:
                            op(A, lambda e: e.copy(out=stg[j][:], in_=psum[:, bk, 0:N]), R=[pb[bk]], W=[stg_b[j]])
                        else:
                            op(V, lambda e: e.tensor_copy(out=stg[j][:], in_=psum[:, bk, 0:N]), R=[pb[bk]], W=[stg_b[j]])
                        dma(s_proj[ci, :, t0 + n * N:t0 + (n + 1) * N], stg[j][:], R=[stg_b[j]], ds=stg_ds[j])
                    r_io.done()
                    bg.slot("io")
            if last:
                nblk = (S + 127) // 128
                for b in range(nblk):
                    nb = min(128, S - b * 128)
                    c0 = b * 128
                    nq = (D + 511) // 512
                    for kc in range(KC):
                        bk = (kc * 128) // 512
                        tb = [res_b[kc][n] for n in range(TPS) if n * N < c0 + nb and (n + 1) * N > c0]
                        op(P, lambda e: e.transpose(psum[0:nb, bk, (kc * 128) % 512:(kc * 128) % 512 + 128], res[:, kc, c0:c0 + nb], ident[:, :]),
                           R=tb + [cst_b], W=[pb[bk]])
                    for q in range(nq):
                        w = min(512, D - q * 512)
                        if q % 2 == 0:
                            op(V, lambda e: e.tensor_copy(out=xt[0:nb, q * 512:q * 512 + w], in_=psum[0:nb, q, 0:w]), R=[pb[q]], W=[xt_b])
                        else:
                            op(A, lambda e: e.copy(out=xt[0:nb, q * 512:q * 512 + w], in_=psum[0:nb, q, 0:w]), R=[pb[q]], W=[xt_b])
                    dma(yout[t0 + c0:t0 + c0 + nb, :], xt[0:nb, :], R=[xt_b], ds=out_ds)
        pg.barrier()
        stack_holder[0].close()
        stack_holder[0] = glob_stack

    def mixer_pass(l):
        stack_holder[0] = ExitStack()
        xr = [sb(f"xr{i}", [128, WB], F32) for i in range(3)]
        xc = sb("xc", [128, SEG], F32); xcb = sb("xcb", [128, SEG], BF16)
        rr = sb("rr", [128, SEG], F32); ii = sb("ii", [128, SEG], F32)
        aa = sb("aa", [128, SEG], F32); mm_ = sb("mm", [128, SEG], F32)
        hf = sb("hf", [128, T], F32); hbk = sb("hbk", [128, SEG], F32)
        t1 = sb("t1", [128, WB], F32); t2 = sb("t2", [128, WB], F32)
        ob = [sb(f"ob{i}", [128, SEG], BF16) for i in range(2)]
        icv = sb("icv", [128, SEG], F32)
        wa = sb("wa", [128, 128], F32); wab = sb("wab", [128, 128], BF16)
        wx = sb("wx", [128, 128], F32); wxb = sb("wxb", [128, 128], BF16)
        car = sb("car", [128, 2], F32)
        xr_b = [Buf() for _ in range(3)]; xr_ds = [pg.dsem() for _ in range(3)]
        xc_b, xcb_b, rr_b, ii_b, aa_b, mm_b, hbk_b, t1_b, t2_b, icv_b = (Buf() for _ in range(10))
        hf_b = [Buf() for _ in range(3)]
        ob_b = [Buf(), Buf()]; ob_ds = [pg.dsem(), pg.dsem()]
        w_b = Buf(); w_ds = pg.dsem(); icv_ds = pg.dsem(); car_b = Buf()
        cn = {"x": 0, "o": 0}
        NTL = SEG // N
        own = slice(H, H + SEG)

        def load_seg(ci, g):
            k = cn["x"] % 3
            cn["x"] += 1
            x, b = xr[k], xr_b[k]
            lo = g * SEG - H
            hi = (g + 1) * SEG + H
            clo, chi = max(lo, 0), min(hi, T)
            if clo > lo:
                op(G, lambda e: e.memset(x[:, 0:clo - lo], 0.0), W=[b])
            if chi < hi:
                op(G, lambda e: e.memset(x[:, WB - (hi - chi):WB], 0.0), W=[b])
            dma(x[:, clo - lo:chi - lo], s_proj[ci, :, clo:chi], W=[b], ds=xr_ds[k])
            if g > 0:
                op(G, lambda e: e.tensor_scalar(out=x[:, 0:H], in0=x[:, 0:H], scalar1=flg[:, g - 1:g], scalar2=None, op0=ALU.mult),
                   R=[b, cst_b], W=[b])
            if g < 2:
                op(G, lambda e: e.tensor_scalar(out=x[:, H + SEG:WB], in0=x[:, H + SEG:WB], scalar1=flg[:, g:g + 1], scalar2=None, op0=ALU.mult),
                   R=[b, cst_b], W=[b])
            if g == 2:
                op(G, lambda e: e.tensor_scalar(out=x[:, H + SEG - 16:H + SEG], in0=x[:, H + SEG - 16:H + SEG], scalar1=flg[:, 2:3], scalar2=None,
                                                op0=ALU.mult), R=[b, cst_b], W=[b])
            return x, b

        def store_out(kc, g, j):
            dma(s_mix[kc, :, g * SEG:(g + 1) * SEG], ob[j][:], R=[ob_b[j]], ds=ob_ds[j])

        for j in range(NCV):
            for g in range(3):
                xB, bB = load_seg(j, g)
                xC, bC = load_seg(NCV + j, g)
                xV, bV = load_seg(2 * NCV + j, g)
                op(G, lambda e: e.tensor_tensor(out=t1[:], in0=xC[:], in1=xV[:], op=ALU.mult), R=[bC, bV], W=[t1_b])
                op(A, lambda e: e.activation(out=xc[:], in_=t1[:, own], func=AF.Identity, scale=pc(f"cw{l}_{j}", 1), bias=pc(f"cb{l}_{j}")),
                   R=[t1_b, par_b], W=[xc_b])
                op(V, lambda e: e.scalar_tensor_tensor(out=xc[:], in0=t1[:, H - 1:H - 1 + SEG], scalar=pc(f"cw{l}_{j}", 0), in1=xc[:],
                                                       op0=ALU.mult, op1=ALU.add), R=[t1_b, par_b, xc_b], W=[xc_b])
                op(V, lambda e: e.scalar_tensor_tensor(out=xc[:], in0=t1[:, H + 1:H + 1 + SEG], scalar=pc(f"cw{l}_{j}", 2), in1=xc[:],
                                                       op0=ALU.mult, op1=ALU.add), R=[t1_b, par_b, xc_b], W=[xc_b])
                k = cn["o"] % 2
                cn["o"] += 1
                op(G, lambda e: e.tensor_tensor(out=ob[k][:], in0=xB[:, own], in1=xc[:], op=ALU.mult), R=[bB, xc_b], W=[ob_b[k]])
                store_out(j, g, k)

        def SL(n):
            return slice(n * N, (n + 1) * N)
        xcS = [Buf() for _ in range(NTL)]; xcbS = [Buf() for _ in range(NTL)]
        rrS = [Buf() for _ in range(NTL)]; iiS = [Buf() for _ in range(NTL)]
        aaS = [Buf() for _ in range(NTL)]; mmS = [Buf() for _ in range(NTL)]
        hbS = [Buf() for _ in range(NTL)]; t1S = [Buf() for _ in range(NTL)]; t2S = [Buf() for _ in range(NTL)]
        obS = [[Buf() for _ in range(NTL)] for _ in range(2)]
        hfS = [[Buf() for _ in range(NTL)] for _ in range(3)]
        class Unit:
            def __init__(u, h, d, g, first):
                u.h, u.d, u.g, u.first = h, d, g, first
                u.ix = (l * 2 + d) * NH + h
                u.lw = f"lw{l}_{d}_{h}"
                u.sgn = -1 if d == 0 else 1
                u.order = list(range(NTL)) if d == 0 else list(range(NTL - 1, -1, -1))

            def p1a(u):
                h, d, g = u.h, u.d, u.g
                u.xX, u.bX = load_seg(3 * NCV + h, g)
                xX, bX = u.xX, u.bX
                for n in u.order:
                    cs = SL(n)
                    op(G, lambda e: e.tensor_scalar(out=xc[:, cs], in0=xX[:, H + n * N:H + (n + 1) * N], scalar1=pc(u.lw, 3),
                                                    scalar2=pc(f"lb{l}_{d}_{h}"), op0=ALU.mult, op1=ALU.add),
                       R=[bX, par_b], W=[xcS[n], xc_b])
                    for kk in (1, 2, 3):
                        o_ = H + u.sgn * kk + n * N
                        op(V, lambda e: e.scalar_tensor_tensor(out=xc[:, cs], in0=xX[:, o_:o_ + N], scalar=pc(u.lw, 3 - kk), in1=xc[:, cs],
                                                               op0=ALU.mult, op1=ALU.add), R=[bX, par_b, xcS[n]], W=[xcS[n]])
                    op(V, lambda e: e.tensor_copy(out=xcb[:, cs], in_=xc[:, cs]), R=[xcS[n]], W=[xcbS[n], xcb_b])

            def p1b(u):
                h, d, g = u.h, u.d, u.g
                if u.first:
                    dma(wa[:], lwa[l, d, h], W=[w_b], ds=w_ds)
                    dma(wx[:], lwx[l, d, h], W=[w_b], ds=w_ds)
                    op(V, lambda e: e.tensor_copy(out=wab[:], in_=wa[:]), R=[w_b], W=[w_b])
                    op(V, lambda e: e.tensor_copy(out=wxb[:], in_=wx[:]), R=[w_b], W=[w_b])
                for n in u.order:
                    cs = SL(n)
                    b1, b2 = 4 + (n % 2), 6 + (n % 2)
                    op(P, lambda e: e.matmul(psum[:, b1, 0:N], lhsT=wab[:], rhs=xcb[:, cs], start=True, stop=True), R=[w_b, xcbS[n]], W=[pb[b1]])
                    op(P, lambda e: e.matmul(psum[:, b2, 0:N], lhsT=wxb[:], rhs=xcb[:, cs], start=True, stop=True), R=[w_b, xcbS[n]], W=[pb[b2]])
                    op(A, lambda e: e.activation(out=rr[:, cs], in_=psum[:, b1, 0:N], func=AF.Sigmoid, bias=pc(f"ba{l}_{d}_{h}"), scale=1.0),
                       R=[pb[b1], par_b], W=[rrS[n]])
                    op(A, lambda e: e.activation(out=ii[:, cs], in_=psum[:, b2, 0:N], func=AF.Sigmoid, bias=pc(f"bx{l}_{d}_{h}"), scale=1.0),
                       R=[pb[b2], par_b], W=[iiS[n]])
                    op(V, lambda e: e.tensor_tensor(out=ii[:, cs], in0=ii[:, cs], in1=xc[:, cs], op=ALU.mult), R=[iiS[n], xcS[n]], W=[iiS[n]])

            def p23(u):
                h, d, g, ix = u.h, u.d, u.g, u.ix
                for n in u.order:
                    cs = SL(n)
                    op(A, lambda e: e.activation(out=aa[:, cs], in_=rr[:, cs], func=AF.Exp, scale=coef[:, 2 * ix:2 * ix + 1]),
                       R=[rrS[n], coef_b], W=[aaS[n]])
                    op(A, lambda e: e.activation(out=mm_[:, cs], in_=rr[:, cs], func=AF.Exp, scale=coef[:, 2 * ix + 1:2 * ix + 2]),
                       R=[rrS[n], coef_b], W=[mmS[n]])
                for n in u.order:
                    cs = SL(n)
                    op(A, lambda e: e.activation(out=mm_[:, cs], in_=mm_[:, cs], func=AF.Sqrt, scale=-1.0, bias=cbias[:, 1:2]),
                       R=[mmS[n], cst_b], W=[mmS[n]])
                if d == 1:
                    u.xG, u.bG = load_seg(3 * NCV + NH + h, g)
                    xG, bG = u.xG, u.bG
                    u.k = cn["o"] % 2
                    cn["o"] += 1
                    for n in u.order:
                        cs = SL(n)
                        gv = xG[:, H + n * N:H + (n + 1) * N]
                        op(V, lambda e: e.tensor_tensor(out=t1[:, cs], in0=gv, in1=gv, op=ALU.mult), R=[bG], W=[t1S[n], t1_b])
                        op(G, lambda e: e.tensor_scalar(out=t1[:, cs], in0=t1[:, cs], scalar1=0.044715, scalar2=1.0, op0=ALU.mult, op1=ALU.add),
                           R=[t1S[n]], W=[t1S[n]])
                        op(V, lambda e: e.tensor_tensor(out=t1[:, cs], in0=t1[:, cs], in1=gv, op=ALU.mult), R=[t1S[n], bG], W=[t1S[n]])
                    for n in u.order:
                        cs = SL(n)
                        op(A, lambda e: e.activation(out=t1[:, cs], in_=t1[:, cs], func=AF.Sigmoid, scale=1.5957691216057308), R=[t1S[n]], W=[t1S[n]])

            def p4(u):
                h, d, g = u.h, u.d, u.g
                for n in u.order:
                    cs = SL(n)
                    op(G, lambda e: e.tensor_tensor(out=ii[:, cs], in0=ii[:, cs], in1=mm_[:, cs], op=ALU.mult), R=[iiS[n], mmS[n]], W=[iiS[n]])
                    if d == 0:
                        gcs = slice(g * SEG + n * N, g * SEG + (n + 1) * N)
                        if n == 0 and g == 0:
                            init = 0.0
                            Rx = []
                        elif n == 0:
                            op(V, lambda e: e.tensor_scalar(out=car[:, 0:1], in0=hf[:, g * SEG - 1:g * SEG], scalar1=flg[:, g - 1:g], scalar2=None,
                                                            op0=ALU.mult), R=[hfS[g - 1][NTL - 1], cst_b], W=[car_b])
                            init = car[:, 0:1]
                            Rx = [car_b]
                        else:
                            init = hf[:, g * SEG + n * N - 1:g * SEG + n * N]
                            Rx = [hfS[g][n - 1]]
                        op(V, lambda e: e.tensor_tensor_scan(out=hf[:, gcs], data0=aa[:, cs], data1=ii[:, cs], initial=init, op0=ALU.mult, op1=ALU.add),
                           R=[aaS[n], iiS[n]] + Rx, W=[hfS[g][n]])
                    else:
                        xG, bG, k = u.xG, u.bG, u.k
                        if g == 2 and n == (SEG - 17) // N:
                            c_ = SEG - 17
                            op(G, lambda e: e.tensor_scalar(out=aa[:, c_:c_ + 1], in0=aa[:, c_:c_ + 1], scalar1=flg[:, 2:3], scalar2=None, op0=ALU.mult),
                               R=[aaS[n], cst_b], W=[aaS[n]])
                        if n == NTL - 1 and g == 2:
                            init = 0.0
                            Rx = []
                        elif n == NTL - 1:
                            op(V, lambda e: e.tensor_scalar(out=car[:, 1:2], in0=hbk[:, 0:1], scalar1=flg[:, g:g + 1], scalar2=None, op0=ALU.mult),
                               R=[hbS[0], cst_b], W=[car_b])
                            init = car[:, 1:2]
                            Rx = [car_b]
                        else:
                            init = hbk[:, (n + 1) * N:(n + 1) * N + 1]
                            Rx = [hbS[n + 1]]
                        r0, r1 = n * N, (n + 1) * N
                        rev = slice(r1 - 1, r0 - 1 if r0 > 0 else None, -1)
                        op(V, lambda e: e.tensor_tensor_scan(out=hbk[:, rev], data0=aa[:, rev], data1=ii[:, rev], initial=init,
                                                             op0=ALU.mult, op1=ALU.add), R=[aaS[n], iiS[n]] + Rx, W=[hbS[n], hbk_b])
                        gv = xG[:, H + n * N:H + (n + 1) * N]
                        gcs = slice(g * SEG + n * N, g * SEG + (n + 1) * N)
                        op(V, lambda e: e.tensor_tensor(out=t1[:, cs], in0=t1[:, cs], in1=gv, op=ALU.mult), R=[t1S[n], bG], W=[t1S[n]])
                        op(V, lambda e: e.tensor_tensor(out=t2[:, cs], in0=hf[:, gcs], in1=hbk[:, cs], op=ALU.add), R=[hfS[g][n], hbS[n]], W=[t2S[n], t2_b])
                        op(G, lambda e: e.tensor_tensor(out=ob[k][:, cs], in0=t1[:, cs], in1=t2[:, cs], op=ALU.mult), R=[t1S[n], t2S[n]],
                           W=[obS[k][n], ob_b[k]])
                if d == 1:
                    dma(s_mix[NCV + h, :, g * SEG:(g + 1) * SEG], ob[u.k][:], R=[ob_b[u.k]] + obS[u.k], ds=ob_ds[u.k])

        units = []
        for h in range(NH):
            for d in range(2):
                gs = [0, 1, 2] if d == 0 else [2, 1, 0]
                for gi, g in enumerate(gs):
                    units.append(Unit(h, d, g, gi == 0))
        units[0].p1a()
        for ui, u in enumerate(units):
            u.p1b()
            if ui + 1 < len(units):
                units[ui + 1].p1a()
            u.p23()
            u.p4()

        join = xcS + xcbS + t1S + t2S + hbS
        for bb in (xc_b, xcb_b, t1_b, t2_b, hbk_b):
            for sbuf_ in join:
                if sbuf_.w is not None:
                    bb.r.append(sbuf_.w)
                bb.r.extend(sbuf_.r)

        for q in range(4):
            cP = 3 * NCV + 2 * NH + q
            dma(wa[:], plw[l, q], W=[w_b], ds=w_ds)
            op(V, lambda e: e.tensor_copy(out=wab[:], in_=wa[:]), R=[w_b], W=[w_b])
            for g in range(3):
                xU, bU = load_seg(cP, g)
                dma(icv[:], invc_d[q, g * SEG:(g + 1) * SEG].partition_broadcast(128), W=[icv_b], ds=icv_ds)
                src, srcb = xU, bU
                width = WB
                dsts = [(t1, t1_b), (t2, t2_b)]
                for lev in range(q + 1):
                    sh = 1 << lev
                    dst, dstb = dsts[lev % 2]
                    nw = width - sh
                    E_ = G if lev % 2 == 0 else V
                    op(E_, lambda e: e.tensor_tensor(out=dst[:, 0:nw], in0=src[:, 0:nw], in1=src[:, sh:sh + nw], op=ALU.add), R=[srcb], W=[dstb])
                    src, srcb, width = dst, dstb, nw
                w2 = (1 << (q + 1)) // 2
                op(V, lambda e: e.tensor_tensor(out=xc[:], in0=src[:, H - w2:H - w2 + SEG], in1=icv[:], op=ALU.mult), R=[srcb, icv_b], W=[xc_b])
                op(G, lambda e: e.tensor_tensor(out=xcb[:], in0=xc[:], in1=xU[:, own], op=ALU.subtract), R=[xc_b, bU], W=[xcb_b])
                k = cn["o"] % 2
                cn["o"] += 1
                for n in range(NTL):
                    b1 = bankB()
                    cs = slice(n * N, (n + 1) * N)
                    op(P, lambda e: e.matmul(psum[:, b1, 0:N], lhsT=wab[:], rhs=xcb[:, cs], start=True, stop=True), R=[w_b, xcb_b], W=[pb[b1]])
                    op(A, lambda e: e.activation(out=ob[k][:, cs], in_=psum[:, b1, 0:N], func=AF.Identity, scale=pc(f"ps{l}_{q}"), bias=cbias[:, 2:3]),
                       R=[pb[b1], par_b, cst_b], W=[ob_b[k]])
                store_out(NCV + NH + q, g, k)
        pg.barrier()
        stack_holder[0].close()
        stack_holder[0] = glob_stack

    token_passes(first=True, l_c=None, l_a=0, last=False)
    for l in range(L):
        mixer_pass(l)
        token_passes(first=False, l_c=l, l_a=(l + 1 if l + 1 < L else None), last=(l == L - 1))
    pg.barrier()
    return nc


def core_sequences(c):
    out = []
    for core in range(8):
        if core < 4:
            out.append([("p", core), ("s", core)])
        else:
            b = 4 + 3 * (core - 4)
            out.append([("p", b), ("p", b + 1), ("p", b + 2)])
    return out


def host_prepare(c, inp):
    xp = np.asarray(inp["x_prompt"], np.float32)
    xs = np.asarray(inp["x_sample"], np.float32)
    meta = np.asarray(inp["meta_tokens"], np.float32)
    par = pack_params(c, inp)
    ident = np.eye(128, dtype=np.float32)
    shared = {"par": par, "ident": ident}
    for k in ("ffn1_w_gate", "ffn1_w_up", "ffn1_w_down", "ffn2_w_gate", "ffn2_w_up", "ffn2_w_down", "w_in", "w_out",
              "lru_w_a", "lru_w_x", "pool_w"):
        shared[k] = np.ascontiguousarray(np.asarray(inp[k], np.float32))
    in_maps = []
    for core, seqs in enumerate(core_sequences(c)):
        xin = np.zeros((c.T, c.D), np.float32)
        invc = np.ones((4, c.T), np.float32)
        pos = 0
        for kind, idx in seqs:
            x = xp[idx] if kind == "p" else xs[idx]
            Lq = x.shape[0] + c.NMETA
            xin[pos:pos + c.NMETA] = meta
            xin[pos + c.NMETA:pos + Lq] = x
            t = np.arange(Lq)
            for q, w in enumerate((2, 4, 8, 16)):
                lo = np.maximum(t - w // 2, 0)
                hi = np.minimum(t + w // 2 - 1, Lq - 1)
                invc[q, pos:pos + Lq] = (1.0 / (hi - lo + 1)).astype(np.float32)
            pos += Lq
        flags = np.zeros((128, 4), np.float32)
        if core < 4:
            flags[:, 1] = 1.0
            flags[:, 2] = 0.0
        else:
            flags[:, 2] = 1.0
        m = dict(shared)
        m.update({"xin": xin, "invc": invc, "flags": flags})
        in_maps.append(m)
    return in_maps


def host_gather(c, results, n_prompt, n_sample):
    yp = np.zeros((n_prompt, c.LP, c.D), np.float32)
    ys = np.zeros((n_sample, c.LS, c.D), np.float32)
    for core, seqs in enumerate(core_sequences(c)):
        y = results[core]["yout"]
        pos = 0
        for kind, idx in seqs:
            Lx = c.LP if kind == "p" else c.LS
            blk = y[pos + c.NMETA:pos + c.NMETA + Lx]
            if kind == "p":
                yp[idx] = blk
            else:
                ys[idx] = blk
            pos += Lx + c.NMETA
    return yp, ys


_NC_CACHE = {}


def run(c, inp):
    key = id(c)
    if key not in _NC_CACHE:
        _NC_CACHE[key] = build_program(c)
    nc = _NC_CACHE[key]
    in_maps = host_prepare(c, inp)
    res = run_bass_kernel_spmd(nc, in_maps, core_ids=list(range(8)))
    return host_gather(c, res.results, inp["x_prompt"].shape[0], inp["x_sample"].shape[0])


def kernel(**inputs):
    yp, ys = run(FULL, inputs)
    return (yp, ys)
```

```python
import numpy as np
import concourse.bass as bass
import concourse.mybir as mybir
from concourse.bass_utils import run_bass_kernel_spmd

F32 = mybir.dt.float32
BF16 = mybir.dt.bfloat16
AF = mybir.ActivationFunctionType
ALU = mybir.AluOpType

H = 8
LN_EPS = 1e-5
LRU_C = 8.0


class Cfg:
    def __init__(self, D, F, FG, NCV, NH, N, TPS, NS, SEGT, DEPTH, LP, LS, NMETA=16):
        self.D, self.F, self.FG, self.NCV, self.NH = D, F, FG, NCV, NH
        self.N, self.TPS, self.NS, self.SEGT, self.DEPTH = N, TPS, NS, SEGT, DEPTH
        self.LP, self.LS, self.NMETA = LP, LS, NMETA
        self.KC = D // 128
        self.FC = F // 128
        self.NG = self.FC // FG
        self.CIN = 3 * NCV + 2 * NH + 4
        self.KM = NCV + NH + 4
        self.S = N * TPS
        self.T = self.S * NS
        self.SEG = N * SEGT
        assert self.T == 3 * self.SEG and self.FC % FG == 0
        assert self.SEG == LP + NMETA and 2 * self.SEG == LS + 2 * NMETA
        self.alpha = float((2 * DEPTH) ** 0.25)


FULL = Cfg(D=2048, F=5632, FG=11, NCV=4, NH=8, N=344, TPS=3, NS=6, SEGT=6, DEPTH=2, LP=2048, LS=4096)


def par_layout(c):
    off = {}
    pos = 0

    def add(name, w):
        nonlocal pos
        off[name] = (pos, w)
        pos += w
    add("lnin_g", c.KC); add("lnin_b", c.KC)
    for l in range(c.DEPTH):
        for k in (1, 2, 3):
            add(f"ln{k}_g{l}", c.KC); add(f"ln{k}_b{l}", c.KC)
        for j in range(c.NCV):
            add(f"cw{l}_{j}", 3); add(f"cb{l}_{j}", 1)
        for d in range(2):
            for h in range(c.NH):
                add(f"lw{l}_{d}_{h}", 4); add(f"lb{l}_{d}_{h}", 1)
                add(f"ba{l}_{d}_{h}", 1); add(f"bx{l}_{d}_{h}", 1); add(f"lam{l}_{d}_{h}", 1)
        for q in range(4):
            add(f"ps{l}_{q}", 1)
    return off, pos


def pack_params(c, inp):
    off, npar = par_layout(c)
    par = np.zeros((128, npar), np.float32)

    def put(name, arr):
        o, w = off[name]
        par[:, o:o + w] = arr

    def cols(v):
        v = np.asarray(v, np.float32)
        return v.reshape(-1, 128).T
    put("lnin_g", cols(inp["ln_in_g"])); put("lnin_b", cols(inp["ln_in_b"]))
    for l in range(c.DEPTH):
        for k in (1, 2, 3):
            put(f"ln{k}_g{l}", cols(inp[f"ln{k}_g"][l])); put(f"ln{k}_b{l}", cols(inp[f"ln{k}_b"][l]))
        for j in range(c.NCV):
            put(f"cw{l}_{j}", np.asarray(inp["conv_w"][l])[:, j * 128:(j + 1) * 128].T)
            put(f"cb{l}_{j}", np.asarray(inp["conv_b"][l])[j * 128:(j + 1) * 128, None])
        for d in range(2):
            for h in range(c.NH):
                sl = slice(h * 128, (h + 1) * 128)
                put(f"lw{l}_{d}_{h}", np.asarray(inp["lru_conv_w"][l, d])[:, sl].T)
                put(f"lb{l}_{d}_{h}", np.asarray(inp["lru_conv_b"][l, d])[sl, None])
                put(f"ba{l}_{d}_{h}", np.asarray(inp["lru_b_a"][l, d])[sl, None])
                put(f"bx{l}_{d}_{h}", np.asarray(inp["lru_b_x"][l, d])[sl, None])
                put(f"lam{l}_{d}_{h}", np.asarray(inp["lru_lambda"][l, d])[sl, None])
        for q in range(4):
            put(f"ps{l}_{q}", np.asarray(inp["pool_scale"][l])[q * 128:(q + 1) * 128, None])
    return par


class Buf:
    __slots__ = ("w", "r")

    def __init__(self):
        self.w = None
        self.r = []


class Eng:
    def __init__(self, nc, h, name, selfsync=True, has_sem=True):
        self.h = h
        self.sem = nc.semaphore(name).__enter__() if has_sem else None
        self.cnt = 0
        self.waited = {}
        self.selfsync = selfsync


class DSem:
    def __init__(self, nc, name):
        self.sem = nc.semaphore(name).__enter__()
        self.cnt = 0


class Prog:
    def __init__(self, nc):
        self.nc = nc
        self.P = Eng(nc, nc.tensor, "sP", selfsync=False)
        self.A = Eng(nc, nc.scalar, "sA")
        self.V = Eng(nc, nc.vector, "sV")
        self.G = Eng(nc, nc.gpsimd, "sG")
        self.Q = Eng(nc, nc.sync, "sQ", has_sem=False)
        self.engs = [self.P, self.A, self.V, self.G, self.Q]
        self.dsems = []
        self._nds = 0

    def dsem(self):
        self._nds += 1
        d = DSem(self.nc, f"d{self._nds}")
        self.dsems.append(d)
        return d

    def _deps(self, E, R, W):
        deps = {}
        for b in R:
            if b.w is not None:
                s, v = b.w
                if deps.get(s, 0) < v:
                    deps[s] = v
        for b in W:
            if b.w is not None:
                s, v = b.w
                if deps.get(s, 0) < v:
                    deps[s] = v
            for (s, v) in b.r:
                if deps.get(s, 0) < v:
                    deps[s] = v
        for s, v in deps.items():
            if s is E.sem and not E.selfsync:
                continue
            if E.waited.get(s, 0) < v:
                E.h.wait_ge(s, v)
                E.waited[s] = v

    def _record(self, tk, R, W):
        for b in R:
            b.r.append(tk)
            if len(b.r) > 64:
                m = {}
                for (s, v) in b.r:
                    if m.get(s, 0) < v:
                        m[s] = v
                b.r = list(m.items())
        for b in W:
            b.w = tk
            b.r = []

    def op(self, E, fn, R=(), W=(), inc=True):
        self._deps(E, R, W)
        ins = fn(E.h)
        if inc:
            E.cnt += 1
            ins.then_inc(E.sem, 1)
            tk = (E.sem, E.cnt)
        else:
            tk = (E.sem, E.cnt + 1)
        self._record(tk, R, W)
        return ins

    def dma(self, out, in_, R=(), W=(), ds=None, E=None, hold=None):
        E = E or self.Q
        self._deps(E, R, W)
        E.h.dma_start(out=out, in_=in_).then_inc(ds.sem, 16)
        ds.cnt += 16
        if hold is not None:
            hold.append((R, W))
        else:
            self._record((ds.sem, ds.cnt), R, W)

    def flush(self, hold, ds):
        for (R, W) in hold:
            self._record((ds.sem, ds.cnt), R, W)
        del hold[:]

    def barrier(self):
        cur = [(e.sem, e.cnt) for e in self.engs if e.sem is not None] + [(d.sem, d.cnt) for d in self.dsems]
        for E in self.engs:
            for s, v in cur:
                if v > 0 and s is not E.sem and E.waited.get(s, 0) < v:
                    E.h.wait_ge(s, v)
                    E.waited[s] = v


class Ring:
    def __init__(self, pg, tiles, loader, items):
        self.pg, self.tiles, self.loader, self.items = pg, tiles, loader, items
        self.R = len(tiles)
        self.bufs = [Buf() for _ in tiles]
        self.ds = [pg.dsem() for _ in tiles]
        self.next_load = 0
        self.next_use = 0

    def _load(self):
        i = self.next_load
        if i >= len(self.items):
            return
        k = i % self.R
        self.loader(self.tiles[k], self.items[i], self.bufs[k], self.ds[k])
        self.next_load += 1

    def prime(self):
        while self.next_load < min(self.R, len(self.items)) and self.next_load - self.next_use < self.R:
            self._load()

    def get(self, item):
        i = self.next_use
        assert self.items[i] == item, (self.items[i], item)
        k = i % self.R
        return self.tiles[k], self.bufs[k]

    def done(self):
        self.next_use += 1
        while self.next_load < len(self.items) and self.next_load - self.next_use < self.R:
            self._load()


def build_program(c, debug=False):
    nc = bass.Bass("TRN2", target_bir_lowering=False)
    KC, FC, FG, NG, N, S, T, SEG, TPS, NS = c.KC, c.FC, c.FG, c.NG, c.N, c.S, c.T, c.SEG, c.TPS, c.NS
    D, F, CIN, KM, NCV, NH, L = c.D, c.F, c.CIN, c.KM, c.NCV, c.NH, c.DEPTH
    DIN = CIN * 128
    DMX = KM * 128
    WB = SEG + 2 * H
    off, NPAR = par_layout(c)

    def din(name, shape, dt=F32):
        return nc.dram_tensor(name, shape, dt, kind="ExternalInput").ap()
    xin = din("xin", [T, D])
    par_d = din("par", [128, NPAR])
    flags_d = din("flags", [128, 4])
    invc_d = din("invc", [4, T])
    ident_d = din("ident", [128, 128])
    w_g = [din("ffn1_w_gate", [L, D, F]), din("ffn2_w_gate", [L, D, F])]
    w_u = [din("ffn1_w_up", [L, D, F]), din("ffn2_w_up", [L, D, F])]
    w_d = [din("ffn1_w_down", [L, F, D]), din("ffn2_w_down", [L, F, D])]
    w_in = din("w_in", [L, D, DIN])
    w_out = din("w_out", [L, DMX, D])
    lwa = din("lru_w_a", [L, 2, NH, 128, 128])
    lwx = din("lru_w_x", [L, 2, NH, 128, 128])
    plw = din("pool_w", [L, 4, 128, 128])
    yout = nc.dram_tensor("yout", [T, D], F32, kind="ExternalOutput").ap()

    def dscr(name, shape, dt):
        if debug and name in ("s_hres0", "s_proj", "s_mix"):
            return nc.dram_tensor(name, shape, dt, kind="ExternalOutput").ap()
        return nc.dram_tensor(name, shape, dt).ap()
    s_gu = [[dscr(f"s_gu{l}_{k}", [FC, 128, 2, KC * 128], BF16) for k in range(2)] for l in range(L)]
    s_d = [[dscr(f"s_d{l}_{k}", [NG, KC, 128, FG * 128], BF16) for k in range(2)] for l in range(L)]
    s_in = [dscr(f"s_in{l}", [CIN, 128, KC * 128], BF16) for l in range(L)]
    s_out = [dscr(f"s_out{l}", [KC, 128, KM * 128], BF16) for l in range(L)]
    s_hres = [dscr(f"s_hres{l}", [KC, 128, T], F32) for l in range(L)]
    s_proj = dscr("s_proj", [CIN, 128, T], F32)
    s_mix = dscr("s_mix", [KM, 128, T], BF16)

    pg = Prog(nc)
    P, A, V, G, Q = pg.P, pg.A, pg.V, pg.G, pg.Q
    op, dma = pg.op, pg.dma

    from contextlib import ExitStack
    stack_holder = [ExitStack()]

    uniq = [0]

    def sb(name, shape, dt):
        uniq[0] += 1
        return stack_holder[0].enter_context(nc.sbuf_tensor(f"{name}_{uniq[0]}", shape, dt))

    glob_stack = stack_holder[0]
    par = sb("par_sb", [128, NPAR], F32); par_b = Buf()
    flg = sb("flg", [128, 4], F32)
    ident = sb("ident_sb", [128, 128], F32)
    onesb = sb("onesb", [128, 128], BF16)
    coef = sb("coef", [128, L * 2 * NH * 2], F32); coef_b = Buf()
    ctmp = sb("ctmp", [128, L * 2 * NH], F32)
    WG = max(KC, KM) * 128
    gu_t = [sb(f"gu{i}", [128, 2, KC * 128], BF16) for i in range(2)]
    d_t = [sb(f"dd{i}", [128, FG * 128], BF16) for i in range(3)]
    io_t = [sb(f"io{i}", [128, WG], BF16) for i in range(2)]
    psum = nc.psum_tensor("psum", [128, 8, 512], F32).__enter__()
    pb = [Buf() for _ in range(8)]
    cst_b = Buf()
    ds0 = pg.dsem()
    grp0 = []
    dma(par[:], par_d, W=[par_b], ds=ds0, hold=grp0)
    dma(flg[:], flags_d, W=[cst_b], ds=ds0, hold=grp0)
    dma(ident[:], ident_d, W=[cst_b], ds=ds0, hold=grp0)
    pg.flush(grp0, ds0)
    op(G, lambda e: e.memset(onesb[:], 1.0 / D), W=[cst_b])
    cbias = sb("cbias", [128, 4], F32)
    op(G, lambda e: e.memset(cbias[:, 0:1], LN_EPS), W=[cst_b])
    op(G, lambda e: e.memset(cbias[:, 1:2], 1.0), W=[cst_b])
    op(G, lambda e: e.memset(cbias[:, 2:3], 0.0), W=[cst_b])

    def pc(name, j=0, w=1):
        o, _ = off[name]
        return par[:, o + j:o + j + w]
    par_al = sb("par_al", [128, NPAR], F32)
    paral_b = Buf()
    op(V, lambda e: e.tensor_scalar(out=par_al[:], in0=par[:], scalar1=c.alpha, scalar2=None, op0=ALU.mult), R=[par_b], W=[paral_b])

    def pca(name, j=0, w=1):
        o, _ = off[name]
        return par_al[:, o + j:o + j + w]

    for l in range(L):
        for d in range(2):
            for h in range(NH):
                i = (l * 2 + d) * NH + h
                op(A, lambda e: e.activation(out=ctmp[:, i:i + 1], in_=pc(f"lam{l}_{d}_{h}"), func=AF.Exp, scale=-1.0),
                   R=[par_b], W=[coef_b])
    op(A, lambda e: e.activation(out=ctmp[:], in_=ctmp[:], func=AF.Ln, bias=cbias[:, 1:2], scale=1.0), R=[coef_b, cst_b], W=[coef_b])
    cview = coef[:].rearrange("p (i two) -> p i two", two=2)
    op(V, lambda e: e.tensor_scalar(out=cview[:, :, 0], in0=ctmp[:], scalar1=-LRU_C, scalar2=None, op0=ALU.mult),
       R=[coef_b], W=[coef_b])
    op(V, lambda e: e.tensor_scalar(out=cview[:, :, 1], in0=ctmp[:], scalar1=-2.0 * LRU_C, scalar2=None, op0=ALU.mult),
       R=[coef_b], W=[coef_b])

    CE = 512
    cis = [sb(f"cv_in{i}", [128, CE], F32) for i in range(3)]
    NCO = 4
    cos = [sb(f"cv_o{i}", [128, CE], BF16) for i in range(NCO)]
    cib, cob = [Buf() for _ in range(3)], [Buf() for _ in range(NCO)]
    cid, cod = [pg.dsem() for _ in range(3)], [pg.dsem() for _ in range(NCO)]
    phase_order = []
    for l in range(L):
        for k in range(2):
            if k == 1:
                phase_order.append(("out", l))
            for g in range(NG):
                phase_order.append(("gu", l, k, g))
                phase_order.append(("d", l, k, g))
            if k == 0:
                phase_order.append(("in", l))
    phase_idx = {p: i for i, p in enumerate(phase_order)}
    phase_bufs = {p: [Buf() for _ in range(NCO)] for p in phase_order}
    csteps = []

    def add_steps(ph, src_fn, dst_fn, a):
        m = CE // 128
        a0 = 0
        while a0 < a:
            an = min(m, a - a0)
            csteps.append((ph, src_fn(a0, an), dst_fn(a0, an), an))
            a0 += an
    for ph in phase_order:
        if ph[0] == "gu":
            _, l, k, g = ph
            for f in range(g * FG, (g + 1) * FG):
                for wi, wsrc in enumerate((w_g, w_u)):
                    add_steps(ph, lambda a0, an, wsrc=wsrc, f=f, l=l, k=k: wsrc[k][l, a0 * 128:(a0 + an) * 128, f * 128:(f + 1) * 128]
                              .rearrange("(kc p) j -> p kc j", p=128),
                              lambda a0, an, f=f, l=l, k=k, wi=wi: s_gu[l][k][f, :, wi, a0 * 128:(a0 + an) * 128], KC)
        elif ph[0] == "d":
            _, l, k, g = ph
            for dch in range(KC):
                add_steps(ph, lambda a0, an, l=l, k=k, g=g, dch=dch: w_d[k][l, (g * FG + a0) * 128:(g * FG + a0 + an) * 128, dch * 128:(dch + 1) * 128]
                          .rearrange("(fc p) j -> p fc j", p=128),
                          lambda a0, an, l=l, k=k, g=g, dch=dch: s_d[l][k][g, dch, :, a0 * 128:(a0 + an) * 128], FG)
        elif ph[0] == "in":
            _, l = ph
            for ci in range(CIN):
                add_steps(ph, lambda a0, an, l=l, ci=ci: w_in[l, a0 * 128:(a0 + an) * 128, ci * 128:(ci + 1) * 128].rearrange("(kc p) j -> p kc j", p=128),
                          lambda a0, an, l=l, ci=ci: s_in[l][ci, :, a0 * 128:(a0 + an) * 128], KC)
        else:
            _, l = ph
            for dch in range(KC):
                add_steps(ph, lambda a0, an, l=l, dch=dch: w_out[l, a0 * 128:(a0 + an) * 128, dch * 128:(dch + 1) * 128].rearrange("(kc p) j -> p kc j", p=128),
                          lambda a0, an, l=l, dch=dch: s_out[l][dch, :, a0 * 128:(a0 + an) * 128], KM)

    class BgConv:
        LOOK = 2
        LB = 2

        def __init__(self):
            self.i = 0
            self.in_issued = 0
            self.out_issued = 0
            self.first_ffn_end = phase_idx[("d", 0, 0, NG - 1)]

        def _issue_in(self, j):
            ph, src3, dst2, an = csteps[j]
            k = j % 3
            dma(cis[k][:, 0:an * 128].rearrange("p (a b) -> p a b", a=an), src3, W=[cib[k]], ds=cid[k])

        def _issue_out(self, j):
            ph, src3, dst2, an = csteps[j]
            k = j % NCO
            dma(dst2, cos[k][:, 0:an * 128], R=[cob[k]], W=[phase_bufs[ph][k]], ds=cod[k])

        def step(self):
            i = self.i
            if i >= len(csteps):
                return False
            while self.in_issued < min(i + 1 + self.LOOK, len(csteps)):
                self._issue_in(self.in_issued)
                self.in_issued += 1
            while self.out_issued < i - self.LB + 1:
                self._issue_out(self.out_issued)
                self.out_issued += 1
            ph, src3, dst2, an = csteps[i]
            n = an * 128
            ki, ko = i % 3, i % NCO
            op(G, lambda e: e.tensor_copy(out=cos[ko][:, 0:n], in_=cis[ki][:, 0:n]), R=[cib[ki]], W=[cob[ko]])
            self.i += 1
            return True

        def flush_out(self):
            while self.out_issued < self.i:
                self._issue_out(self.out_issued)
                self.out_issued += 1

        def steps(self, n):
            for _ in range(n):
                if not self.step():
                    break
            if self.i >= len(csteps):
                self.flush_out()

        def slot(self, kind):
            if self.i >= len(csteps):
                return
            early = phase_idx[csteps[self.i][0]] <= self.first_ffn_end
            sub = (KC + 3) // 4
            if kind == "gu":
                self.steps(2 * sub if early else sub)
            elif kind == "d":
                self.steps((FG + 3) // 4 if early else max(1, (FG + 3) // 6))
            else:
                self.steps(sub if early else max(1, sub // 2))

        def ensure(self, ph):
            pi = phase_idx[ph]
            while self.i < len(csteps) and phase_idx[csteps[self.i][0]] <= pi:
                self.step()
            self.flush_out()
    bg = BgConv()

    gu_items, d_items, io_items = [], [], []

    def plan_ffn(l, k):
        for g in range(NG):
            for f in range(g * FG, (g + 1) * FG):
                gu_items.append((l, k, f))
            for dch in range(KC):
                d_items.append((l, k, g, dch))

    def plan_A(l):
        plan_ffn(l, 0)
        for ci in range(CIN):
            io_items.append(("in", l, ci))

    def plan_C(l):
        for dch in range(KC):
            io_items.append(("out", l, dch))
        plan_ffn(l, 1)
    for s in range(NS):
        plan_A(0)
    for l in range(L):
        for s in range(NS):
            plan_C(l)
            if l + 1 < L:
                plan_A(l + 1)

    def load_gu(tile, it, b, ds):
        l, k, f = it
        ph = ("gu", l, k, f // FG)
        bg.ensure(ph)
        dma(tile[:], s_gu[l][k][f], R=phase_bufs[ph], W=[b], ds=ds)

    def load_d(tile, it, b, ds):
        l, k, g, dch = it
        ph = ("d", l, k, g)
        bg.ensure(ph)
        dma(tile[:], s_d[l][k][g, dch], R=phase_bufs[ph], W=[b], ds=ds)

    def load_io(tile, it, b, ds):
        kind, l, ci = it
        ph = (kind, l)
        bg.ensure(ph)
        if kind == "in":
            dma(tile[:, 0:KC * 128], s_in[l][ci], R=phase_bufs[ph], W=[b], ds=ds)
        else:
            dma(tile[:, 0:KM * 128], s_out[l][ci], R=phase_bufs[ph], W=[b], ds=ds)
    r_gu = Ring(pg, gu_t, load_gu, gu_items)
    r_d = Ring(pg, d_t, load_d, d_items)
    r_io = Ring(pg, io_t, load_io, io_items)
    r_gu.prime(); r_d.prime()

    pA_rot = [0]
    pB_rot = [0]

    def bankB():
        k = 4 + (pB_rot[0] % 2)
        pB_rot[0] += 1
        return k

    def token_passes(first, l_c, l_a, last):
        stack_holder[0] = ExitStack()
        res = sb("res", [128, KC, S], F32)
        hb = sb("hb", [128, max(KC, KM), S], BF16)
        hT = sb("hT", [128, FG, S], BF16)
        xt = sb("xt", [128, D], F32)
        ybt = [sb(f"ybt{i}", [128, min(4, KC), N], BF16) for i in range(2)]
        yst = [sb(f"yst{i}", [128, min(4, KC), N], BF16) for i in range(2)]
        means = [sb(f"mean{i}", [128, N], F32) for i in range(TPS)]
        rstds = [sb(f"rstd{i}", [128, N], F32) for i in range(TPS)]
        mean_bs = [Buf() for _ in range(TPS)]; rstd_bs = [Buf() for _ in range(TPS)]
        sg = [sb(f"sg{i}", [128, N], F32) for i in range(2)]
        stg = [sb(f"stg{i}", [128, N], F32) for i in range(3)]
        bst = sb("bst", [128, 4 * 6], F32); bag = sb("bag", [128, 2], F32); brs = sb("brs", [128, 1], F32)
        res_b = [[Buf() for _ in range(TPS)] for _ in range(KC)]
        hb_b = [[Buf() for _ in range(TPS)] for _ in range(max(KC, KM))]
        hT_b = [[Buf() for _ in range(TPS)] for _ in range(FG)]
        xt_b = Buf(); ybt_b = [Buf(), Buf()]; yst_b = [Buf(), Buf()]
        sg_b = [Buf(), Buf()]; stg_b = [Buf() for _ in range(3)]
        stg_ds = [pg.dsem() for _ in range(3)]
        bn_b = Buf()
        xt_ds = pg.dsem(); act_ds = pg.dsem(); act2_ds = pg.dsem(); hres_ds = pg.dsem(); out_ds = pg.dsem()
        cnts = {"ln": 0, "stg": 0}

        def ts(n):
            return slice(n * N, (n + 1) * N)

        KG = min(4, KC)

        def layer_norm(gname, bname, scaled=True, need_hb=True):
            pcs = pca if scaled else pc
            assert TPS <= 3
            for n in range(TPS):
                pm, pe2 = 2 * n, 2 * n + 1
                for gk in range(KC // KG):
                    ks = slice(gk * KG, (gk + 1) * KG)
                    rb = [res_b[kc][n] for kc in range(gk * KG, (gk + 1) * KG)]
                    j = cnts["ln"] % 2
                    cnts["ln"] += 1
                    op(V, lambda e: e.tensor_copy(out=ybt[j][:], in_=res[:, ks, ts(n)]), R=rb, W=[ybt_b[j]])
                    op(A, lambda e: e.activation(out=yst[j][:], in_=res[:, ks, ts(n)], func=AF.Square), R=rb, W=[yst_b[j]])
                    for q in range(KG):
                        kc = gk * KG + q
                        op(P, lambda e: e.matmul(psum[:, pm, 0:N], lhsT=onesb[:], rhs=ybt[j][:, q, :], start=(kc == 0), stop=(kc == KC - 1)),
                           R=[ybt_b[j], cst_b], W=[pb[pm]])
                        op(P, lambda e: e.matmul(psum[:, pe2, 0:N], lhsT=onesb[:], rhs=yst[j][:, q, :], start=(kc == 0), stop=(kc == KC - 1)),
                           R=[yst_b[j], cst_b], W=[pb[pe2]])
            for n in range(TPS):
                pm, pe2 = 2 * n, 2 * n + 1
                mean, rstd, mean_b, rstd_b = means[n], rstds[n], mean_bs[n], rstd_bs[n]
                op(V, lambda e: e.tensor_copy(out=mean[:], in_=psum[:, pm, 0:N]), R=[pb[pm]], W=[mean_b])
                op(V, lambda e: e.tensor_tensor(out=rstd[:], in0=mean[:], in1=mean[:], op=ALU.mult), R=[mean_b], W=[rstd_b])
                op(V, lambda e: e.tensor_tensor(out=rstd[:], in0=psum[:, pe2, 0:N], in1=rstd[:], op=ALU.subtract), R=[pb[pe2], rstd_b], W=[rstd_b])
                op(A, lambda e: e.activation(out=rstd[:], in_=rstd[:], func=AF.Sqrt, bias=cbias[:, 0:1], scale=1.0), R=[rstd_b, cst_b], W=[rstd_b])
                op(V, lambda e: e.reciprocal(out=rstd[:], in_=rstd[:]), R=[rstd_b], W=[rstd_b])
            for n in range(TPS):
                mean, rstd, mean_b, rstd_b = means[n], rstds[n], mean_bs[n], rstd_bs[n]
                mean_bc = mean[:].unsqueeze(1).to_broadcast([128, KG, N])
                rstd_bc = rstd[:].unsqueeze(1).to_broadcast([128, KG, N])
                for gk in range(KC // KG):
                    ks = slice(gk * KG, (gk + 1) * KG)
                    rb = [res_b[kc][n] for kc in range(gk * KG, (gk + 1) * KG)]
                    Esub = G if bg.i >= len(csteps) else V
                    op(Esub, lambda e: e.tensor_tensor(out=res[:, ks, ts(n)], in0=res[:, ks, ts(n)], in1=mean_bc, op=ALU.subtract),
                       R=rb + [mean_b], W=rb)
                    op(V, lambda e: e.tensor_tensor(out=res[:, ks, ts(n)], in0=res[:, ks, ts(n)], in1=rstd_bc, op=ALU.mult),
                       R=rb + [rstd_b], W=rb)
                    for kc in range(gk * KG, (gk + 1) * KG):
                        if need_hb:
                            op(A, lambda e: e.activation(out=hb[:, kc, ts(n)], in_=res[:, kc, ts(n)], func=AF.Identity, scale=pc(gname, kc),
                                                         bias=pc(bname, kc)), R=[res_b[kc][n], par_b], W=[hb_b[kc][n]])
                        if kc % 2 == 0:
                            op(V, lambda e: e.tensor_scalar(out=res[:, kc, ts(n)], in0=res[:, kc, ts(n)], scalar1=pcs(gname, kc), scalar2=pcs(bname, kc),
                                                            op0=ALU.mult, op1=ALU.add), R=[res_b[kc][n], par_b, paral_b], W=[res_b[kc][n]])
                        else:
                            op(A, lambda e: e.activation(out=res[:, kc, ts(n)], in_=res[:, kc, ts(n)], func=AF.Identity, scale=pcs(gname, kc),
                                                         bias=pcs(bname, kc)), R=[res_b[kc][n], par_b, paral_b], W=[res_b[kc][n]])

        def ffn(l, k):
            for g in range(NG):
                for f in range(g * FG, (g + 1) * FG):
                    fl = f - g * FG
                    wt, wb = r_gu.get((l, k, f))
                    for n in range(TPS):
                        p0 = 2 * (pA_rot[0] % 2)
                        pA_rot[0] += 1
                        for kc in range(KC):
                            op(P, lambda e: e.matmul(psum[:, p0, 0:N], lhsT=wt[:, 0, kc * 128:(kc + 1) * 128], rhs=hb[:, kc, ts(n)],
                                                     start=(kc == 0), stop=(kc == KC - 1)), R=[wb, hb_b[kc][n]], W=[pb[p0]], inc=(kc == KC - 1))
                        for kc in range(KC):
                            op(P, lambda e: e.matmul(psum[:, p0 + 1, 0:N], lhsT=wt[:, 1, kc * 128:(kc + 1) * 128], rhs=hb[:, kc, ts(n)],
                                                     start=(kc == 0), stop=(kc == KC - 1)), R=[wb, hb_b[kc][n]], W=[pb[p0 + 1]], inc=(kc == KC - 1))
                        j = (p0 // 2)
                        op(A, lambda e: e.activation(out=sg[j][:], in_=psum[:, p0, 0:N], func=AF.Silu), R=[pb[p0]], W=[sg_b[j]])
                        op(V, lambda e: e.tensor_tensor(out=hT[:, fl, ts(n)], in0=psum[:, p0 + 1, 0:N], in1=sg[j][:], op=ALU.mult),
                           R=[pb[p0 + 1], sg_b[j]], W=[hT_b[fl][n]])
                    r_gu.done()
                    bg.slot("gu")
                for dch in range(KC):
                    wt, wb = r_d.get((l, k, g, dch))
                    for n in range(TPS):
                        bk = bankB()
                        for fl in range(FG):
                            op(P, lambda e: e.matmul(psum[:, bk, 0:N], lhsT=wt[:, fl * 128:(fl + 1) * 128], rhs=hT[:, fl, ts(n)],
                                                     start=(fl == 0), stop=(fl == FG - 1)), R=[wb, hT_b[fl][n]], W=[pb[bk]], inc=(fl == FG - 1))
                        op(V, lambda e: e.scalar_tensor_tensor(out=res[:, dch, ts(n)], in0=psum[:, bk, 0:N], scalar=0.5, in1=res[:, dch, ts(n)],
                                                               op0=ALU.mult, op1=ALU.add), R=[pb[bk], res_b[dch][n]], W=[res_b[dch][n]])
                    r_d.done()
                    bg.slot("d")

        for s in range(NS):
            t0 = s * S
            if first:
                nblk = (S + 127) // 128
                for b in range(nblk):
                    nb = min(128, S - b * 128)
                    c0 = b * 128
                    dma(xt[0:nb, :], xin[t0 + c0:t0 + c0 + nb, :], W=[xt_b], ds=xt_ds)
                    nch = (D + 511) // 512
                    for q in range(nch):
                        w = min(512, D - q * 512)
                        op(V, lambda e: e.bn_stats(out=bst[0:nb, q * 6:(q + 1) * 6], in_=xt[0:nb, q * 512:q * 512 + w]), R=[xt_b], W=[bn_b])
                    op(V, lambda e: e.bn_aggr(out=bag[0:nb, :], in_=bst[0:nb, 0:nch * 6]), R=[bn_b], W=[bn_b])
                    op(A, lambda e: e.activation(out=brs[0:nb, :], in_=bag[0:nb, 1:2], func=AF.Sqrt, bias=cbias[0:nb, 0:1], scale=1.0),
                       R=[bn_b, cst_b], W=[bn_b])
                    op(V, lambda e: e.reciprocal(out=brs[0:nb, :], in_=brs[0:nb, :]), R=[bn_b], W=[bn_b])
                    op(V, lambda e: e.tensor_scalar(out=xt[0:nb, :], in0=xt[0:nb, :], scalar1=bag[0:nb, 0:1], scalar2=brs[0:nb, 0:1],
                                                    op0=ALU.subtract, op1=ALU.mult), R=[xt_b, bn_b], W=[xt_b])
                    for kc in range(KC):
                        bk = bankB()
                        op(P, lambda e: e.transpose(psum[:, bk, 0:nb], xt[0:nb, kc * 128:(kc + 1) * 128], ident[0:nb, 0:nb]),
                           R=[xt_b, cst_b], W=[pb[bk]])
                        tb = [res_b[kc][n] for n in range(TPS) if n * N < c0 + nb and (n + 1) * N > c0]
                        op(V, lambda e: e.tensor_scalar(out=res[:, kc, c0:c0 + nb], in0=psum[:, bk, 0:nb], scalar1=pca("lnin_g", kc),
                                                        scalar2=pca("lnin_b", kc), op0=ALU.mult, op1=ALU.add), R=[pb[bk], par_b, paral_b], W=tb)
                        tbh = [hb_b[kc][n] for n in range(TPS) if n * N < c0 + nb and (n + 1) * N > c0]
                        op(A, lambda e: e.activation(out=hb[:, kc, c0:c0 + nb], in_=res[:, kc, c0:c0 + nb], func=AF.Copy, scale=1.0 / c.alpha),
                           R=tb, W=tbh)
            if l_c is not None:
                l = l_c
                grp = []
                for kc in range(KM):
                    dma(hb[:, kc, :], s_mix[kc, :, t0:t0 + S], W=hb_b[kc], ds=act_ds, hold=grp)
                pg.flush(grp, act_ds)
                for kc in range(KC):
                    dma(res[:, kc, :], s_hres[l][kc, :, t0:t0 + S], W=res_b[kc], ds=act2_ds, hold=grp)
                pg.flush(grp, act2_ds)
                for dch in range(KC):
                    wt, wb = r_io.get(("out", l, dch))
                    for n in range(TPS):
                        bk = bankB()
                        for kc in range(KM):
                            op(P, lambda e: e.matmul(psum[:, bk, 0:N], lhsT=wt[:, kc * 128:(kc + 1) * 128], rhs=hb[:, kc, ts(n)],
                                                     start=(kc == 0), stop=(kc == KM - 1)), R=[wb, hb_b[kc][n]], W=[pb[bk]], inc=(kc == KM - 1))
                        op(V, lambda e: e.tensor_tensor(out=res[:, dch, ts(n)], in0=psum[:, bk, 0:N], in1=res[:, dch, ts(n)], op=ALU.add),
                           R=[pb[bk], res_b[dch][n]], W=[res_b[dch][n]])
                    r_io.done()
                    bg.slot("io")
                layer_norm(f"ln2_g{l}", f"ln2_b{l}")
                ffn(l, 1)
                layer_norm(f"ln3_g{l}", f"ln3_b{l}", scaled=not last, need_hb=not last)
            if l_a is not None:
                l = l_a
                ffn(l, 0)
                r_io.prime()
                layer_norm(f"ln1_g{l}", f"ln1_b{l}")
                grp = []
                for kc in range(KC):
                    dma(s_hres[l][kc, :, t0:t0 + S], res[:, kc, :], R=res_b[kc], ds=hres_ds, hold=grp)
                pg.flush(grp, hres_ds)
                for ci in range(CIN):
                    wt, wb = r_io.get(("in", l, ci))
                    for n in range(TPS):
                        bk = bankB()
                        for kc in range(KC):
                            op(P, lambda e: e.matmul(psum[:, bk, 0:N], lhsT=wt[:, kc * 128:(kc + 1) * 128], rhs=hb[:, kc, ts(n)],
                                                     start=(kc == 0), stop=(kc == KC - 1)), R=[wb, hb_b[kc][n]], W=[pb[bk]], inc=(kc == KC - 1))
                        j = cnts["stg"] % 3
                        cnts["stg"] += 1
                        if j == 1:
                            op(A, lambda e: e.copy(out=stg[j][:], in_=psum[:, bk, 0:N]), R=[pb[bk]], W=[stg_b[j]])
                        else:
                            op(V, lambda e: e.tensor_copy(out=stg[j][:], in_=psum[:, bk, 0:N]), R=[pb[bk]], W=[stg_b[j]])
                        dma(s_proj[ci, :, t0 + n * N:t0 + (n + 1) * N], stg[j][:], R=[stg_b[j]], ds=stg_ds[j])
                    r_io.done()
                    bg.slot("io")
            if last:
                nblk = (S + 127) // 128
                for b in range(nblk):
                    nb = min(128, S - b * 128)
                    c0 = b * 128
                    nq = (D + 511) // 512
                    for kc in range(KC):
                        bk = (kc * 128) // 512
                        tb = [res_b[kc][n] for n in range(TPS) if n * N < c0 + nb and (n + 1) * N > c0]
                        op(P, lambda e: e.transpose(psum[0:nb, bk, (kc * 128) % 512:(kc * 128) % 512 + 128], res[:, kc, c0:c0 + nb], ident[:, :]),
                           R=tb + [cst_b], W=[pb[bk]])
                    for q in range(nq):
                        w = min(512, D - q * 512)
                        if q % 2 == 0:
                            op(V, lambda e: e.tensor_copy(out=xt[0:nb, q * 512:q * 512 + w], in_=psum[0:nb, q, 0:w]), R=[pb[q]], W=[xt_b])
                        else:
                            op(A, lambda e: e.copy(out=xt[0:nb, q * 512:q * 512 + w], in_=psum[0:nb, q, 0:w]), R=[pb[q]], W=[xt_b])
                    dma(yout[t0 + c0:t0 + c0 + nb, :], xt[0:nb, :], R=[xt_b], ds=out_ds)
        pg.barrier()
        stack_holder[0].close()
        stack_holder[0] = glob_stack

    def mixer_pass(l):
        stack_holder[0] = ExitStack()
        xr = [sb(f"xr{i}", [128, WB], F32) for i in range(3)]
        xc = sb("xc", [128, SEG], F32); xcb = sb("xcb", [128, SEG], BF16)
        rr = sb("rr", [128, SEG], F32); ii = sb("ii", [128, SEG], F32)
        aa = sb("aa", [128, SEG], F32); mm_ = sb("mm", [128, SEG], F32)
        hf = sb("hf", [128, T], F32); hbk = sb("hbk", [128, SEG], F32)
        t1 = sb("t1", [128, WB], F32); t2 = sb("t2", [128, WB], F32)
        ob = [sb(f"ob{i}", [128, SEG], BF16) for i in range(2)]
        icv = sb("icv", [128, SEG], F32)
        wa = sb("wa", [128, 128], F32); wab = sb("wab", [128, 128], BF16)
        wx = sb("wx", [128, 128], F32); wxb = sb("wxb", [128, 128], BF16)
        car = sb("car", [128, 2], F32)
        xr_b = [Buf() for _ in range(3)]; xr_ds = [pg.dsem() for _ in range(3)]
        xc_b, xcb_b, rr_b, ii_b, aa_b, mm_b, hbk_b, t1_b, t2_b, icv_b = (Buf() for _ in range(10))
        hf_b = [Buf() for _ in range(3)]
        ob_b = [Buf(), Buf()]; ob_ds = [pg.dsem(), pg.dsem()]
        w_b = Buf(); w_ds = pg.dsem(); icv_ds = pg.dsem(); car_b = Buf()
        cn = {"x": 0, "o": 0}
        NTL = SEG // N
        own = slice(H, H + SEG)

        def load_seg(ci, g):
            k = cn["x"] % 3
            cn["x"] += 1
            x, b = xr[k], xr_b[k]
            lo = g * SEG - H
            hi = (g + 1) * SEG + H
            clo, chi = max(lo, 0), min(hi, T)
            if clo > lo:
                op(G, lambda e: e.memset(x[:, 0:clo - lo], 0.0), W=[b])
            if chi < hi:
                op(G, lambda e: e.memset(x[:, WB - (hi - chi):WB], 0.0), W=[b])
            dma(x[:, clo - lo:chi - lo], s_proj[ci, :, clo:chi], W=[b], ds=xr_ds[k])
            if g > 0:
                op(G, lambda e: e.tensor_scalar(out=x[:, 0:H], in0=x[:, 0:H], scalar1=flg[:, g - 1:g], scalar2=None, op0=ALU.mult),
                   R=[b, cst_b], W=[b])
            if g < 2:
                op(G, lambda e: e.tensor_scalar(out=x[:, H + SEG:WB], in0=x[:, H + SEG:WB], scalar1=flg[:, g:g + 1], scalar2=None, op0=ALU.mult),
                   R=[b, cst_b], W=[b])
            if g == 2:
                op(G, lambda e: e.tensor_scalar(out=x[:, H + SEG - 16:H + SEG], in0=x[:, H + SEG - 16:H + SEG], scalar1=flg[:, 2:3], scalar2=None,
                                                op0=ALU.mult), R=[b, cst_b], W=[b])
            return x, b

        def store_out(kc, g, j):
            dma(s_mix[kc, :, g * SEG:(g + 1) * SEG], ob[j][:], R=[ob_b[j]], ds=ob_ds[j])

        for j in range(NCV):
            for g in range(3):
                xB, bB = load_seg(j, g)
                xC, bC = load_seg(NCV + j, g)
                xV, bV = load_seg(2 * NCV + j, g)
                op(G, lambda e: e.tensor_tensor(out=t1[:], in0=xC[:], in1=xV[:], op=ALU.mult), R=[bC, bV], W=[t1_b])
                op(A, lambda e: e.activation(out=xc[:], in_=t1[:, own], func=AF.Identity, scale=pc(f"cw{l}_{j}", 1), bias=pc(f"cb{l}_{j}")),
                   R=[t1_b, par_b], W=[xc_b])
                op(V, lambda e: e.scalar_tensor_tensor(out=xc[:], in0=t1[:, H - 1:H - 1 + SEG], scalar=pc(f"cw{l}_{j}", 0), in1=xc[:],
                                                       op0=ALU.mult, op1=ALU.add), R=[t1_b, par_b, xc_b], W=[xc_b])
                op(V, lambda e: e.scalar_tensor_tensor(out=xc[:], in0=t1[:, H + 1:H + 1 + SEG], scalar=pc(f"cw{l}_{j}", 2), in1=xc[:],
                                                       op0=ALU.mult, op1=ALU.add), R=[t1_b, par_b, xc_b], W=[xc_b])
                k = cn["o"] % 2
                cn["o"] += 1
                op(G, lambda e: e.tensor_tensor(out=ob[k][:], in0=xB[:, own], in1=xc[:], op=ALU.mult), R=[bB, xc_b], W=[ob_b[k]])
                store_out(j, g, k)

        def SL(n):
            return slice(n * N, (n + 1) * N)
        xcS = [Buf() for _ in range(NTL)]; xcbS = [Buf() for _ in range(NTL)]
        rrS = [Buf() for _ in range(NTL)]; iiS = [Buf() for _ in range(NTL)]
        aaS = [Buf() for _ in range(NTL)]; mmS = [Buf() for _ in range(NTL)]
        hbS = [Buf() for _ in range(NTL)]; t1S = [Buf() for _ in range(NTL)]; t2S = [Buf() for _ in range(NTL)]
        obS = [[Buf() for _ in range(NTL)] for _ in range(2)]
        hfS = [[Buf() for _ in range(NTL)] for _ in range(3)]
        class Unit:
            def __init__(u, h, d, g, first):
                u.h, u.d, u.g, u.first = h, d, g, first
                u.ix = (l * 2 + d) * NH + h
                u.lw = f"lw{l}_{d}_{h}"
                u.sgn = -1 if d == 0 else 1
                u.order = list(range(NTL)) if d == 0 else list(range(NTL - 1, -1, -1))

            def p1a(u):
                h, d, g = u.h, u.d, u.g
                u.xX, u.bX = load_seg(3 * NCV + h, g)
                xX, bX = u.xX, u.bX
                for n in u.order:
                    cs = SL(n)
                    op(G, lambda e: e.tensor_scalar(out=xc[:, cs], in0=xX[:, H + n * N:H + (n + 1) * N], scalar1=pc(u.lw, 3),
                                                    scalar2=pc(f"lb{l}_{d}_{h}"), op0=ALU.mult, op1=ALU.add),
                       R=[bX, par_b], W=[xcS[n], xc_b])
                    for kk in (1, 2, 3):
                        o_ = H + u.sgn * kk + n * N
                        op(V, lambda e: e.scalar_tensor_tensor(out=xc[:, cs], in0=xX[:, o_:o_ + N], scalar=pc(u.lw, 3 - kk), in1=xc[:, cs],
                                                               op0=ALU.mult, op1=ALU.add), R=[bX, par_b, xcS[n]], W=[xcS[n]])
                    op(G, lambda e: e.tensor_copy(out=xcb[:, cs], in_=xc[:, cs]), R=[xcS[n]], W=[xcbS[n], xcb_b])

            def p1b(u):
                h, d, g = u.h, u.d, u.g
                if u.first:
                    dma(wa[:], lwa[l, d, h], W=[w_b], ds=w_ds)
                    dma(wx[:], lwx[l, d, h], W=[w_b], ds=w_ds)
                    op(V, lambda e: e.tensor_copy(out=wab[:], in_=wa[:]), R=[w_b], W=[w_b])
                    op(V, lambda e: e.tensor_copy(out=wxb[:], in_=wx[:]), R=[w_b], W=[w_b])
                for n in u.order:
                    cs = SL(n)
                    b1, b2 = 4 + (n % 2), 6 + (n % 2)
                    op(P, lambda e: e.matmul(psum[:, b1, 0:N], lhsT=wab[:], rhs=xcb[:, cs], start=True, stop=True), R=[w_b, xcbS[n]], W=[pb[b1]])
                    op(P, lambda e: e.matmul(psum[:, b2, 0:N], lhsT=wxb[:], rhs=xcb[:, cs], start=True, stop=True), R=[w_b, xcbS[n]], W=[pb[b2]])
                    op(A, lambda e: e.activation(out=rr[:, cs], in_=psum[:, b1, 0:N], func=AF.Sigmoid, bias=pc(f"ba{l}_{d}_{h}"), scale=1.0),
                       R=[pb[b1], par_b], W=[rrS[n]])
                    op(A, lambda e: e.activation(out=ii[:, cs], in_=psum[:, b2, 0:N], func=AF.Sigmoid, bias=pc(f"bx{l}_{d}_{h}"), scale=1.0),
                       R=[pb[b2], par_b], W=[iiS[n]])
                    op(V, lambda e: e.tensor_tensor(out=ii[:, cs], in0=ii[:, cs], in1=xc[:, cs], op=ALU.mult), R=[iiS[n], xcS[n]], W=[iiS[n]])

            def p23(u):
                h, d, g, ix = u.h, u.d, u.g, u.ix
                for n in u.order:
                    cs = SL(n)
                    op(A, lambda e: e.activation(out=aa[:, cs], in_=rr[:, cs], func=AF.Exp, scale=coef[:, 2 * ix:2 * ix + 1]),
                       R=[rrS[n], coef_b], W=[aaS[n]])
                    op(A, lambda e: e.activation(out=mm_[:, cs], in_=rr[:, cs], func=AF.Exp, scale=coef[:, 2 * ix + 1:2 * ix + 2]),
                       R=[rrS[n], coef_b], W=[mmS[n]])
                for n in u.order:
                    cs = SL(n)
                    op(A, lambda e: e.activation(out=mm_[:, cs], in_=mm_[:, cs], func=AF.Sqrt, scale=-1.0, bias=cbias[:, 1:2]),
                       R=[mmS[n], cst_b], W=[mmS[n]])
                if d == 1:
                    u.xG, u.bG = load_seg(3 * NCV + NH + h, g)
                    xG, bG = u.xG, u.bG
                    u.k = cn["o"] % 2
                    cn["o"] += 1
                    for n in u.order:
                        cs = SL(n)
                        gv = xG[:, H + n * N:H + (n + 1) * N]
                        op(G, lambda e: e.tensor_tensor(out=t1[:, cs], in0=gv, in1=gv, op=ALU.mult), R=[bG], W=[t1S[n], t1_b])
                        op(G, lambda e: e.tensor_scalar(out=t1[:, cs], in0=t1[:, cs], scalar1=0.044715, scalar2=1.0, op0=ALU.mult, op1=ALU.add),
                           R=[t1S[n]], W=[t1S[n]])
                        op(V, lambda e: e.tensor_tensor(out=t1[:, cs], in0=t1[:, cs], in1=gv, op=ALU.mult), R=[t1S[n], bG], W=[t1S[n]])
                    for n in u.order:
                        cs = SL(n)
                        op(A, lambda e: e.activation(out=t1[:, cs], in_=t1[:, cs], func=AF.Sigmoid, scale=1.5957691216057308), R=[t1S[n]], W=[t1S[n]])

            def p4(u):
                h, d, g = u.h, u.d, u.g
                for n in u.order:
                    cs = SL(n)
                    op(G, lambda e: e.tensor_tensor(out=ii[:, cs], in0=ii[:, cs], in1=mm_[:, cs], op=ALU.mult), R=[iiS[n], mmS[n]], W=[iiS[n]])
                    if d == 0:
                        gcs = slice(g * SEG + n * N, g * SEG + (n + 1) * N)
                        if n == 0 and g == 0:
                            init = 0.0
                            Rx = []
                        elif n == 0:
                            op(V, lambda e: e.tensor_scalar(out=car[:, 0:1], in0=hf[:, g * SEG - 1:g * SEG], scalar1=flg[:, g - 1:g], scalar2=None,
                                                            op0=ALU.mult), R=[hfS[g - 1][NTL - 1], cst_b], W=[car_b])
                            init = car[:, 0:1]
                            Rx = [car_b]
                        else:
                            init = hf[:, g * SEG + n * N - 1:g * SEG + n * N]
                            Rx = [hfS[g][n - 1]]
                        op(V, lambda e: e.tensor_tensor_scan(out=hf[:, gcs], data0=aa[:, cs], data1=ii[:, cs], initial=init, op0=ALU.mult, op1=ALU.add),
                           R=[aaS[n], iiS[n]] + Rx, W=[hfS[g][n]])
                    else:
                        xG, bG, k = u.xG, u.bG, u.k
                        if g == 2 and n == (SEG - 17) // N:
                            c_ = SEG - 17
                            op(G, lambda e: e.tensor_scalar(out=aa[:, c_:c_ + 1], in0=aa[:, c_:c_ + 1], scalar1=flg[:, 2:3], scalar2=None, op0=ALU.mult),
                               R=[aaS[n], cst_b], W=[aaS[n]])
                        if n == NTL - 1 and g == 2:
                            init = 0.0
                            Rx = []
                        elif n == NTL - 1:
                            op(V, lambda e: e.tensor_scalar(out=car[:, 1:2], in0=hbk[:, 0:1], scalar1=flg[:, g:g + 1], scalar2=None, op0=ALU.mult),
                               R=[hbS[0], cst_b], W=[car_b])
                            init = car[:, 1:2]
                            Rx = [car_b]
                        else:
                            init = hbk[:, (n + 1) * N:(n + 1) * N + 1]
                            Rx = [hbS[n + 1]]
                        r0, r1 = n * N, (n + 1) * N
                        rev = slice(r1 - 1, r0 - 1 if r0 > 0 else None, -1)
                        op(V, lambda e: e.tensor_tensor_scan(out=hbk[:, rev], data0=aa[:, rev], data1=ii[:, rev], initial=init,
                                                             op0=ALU.mult, op1=ALU.add), R=[aaS[n], iiS[n]] + Rx, W=[hbS[n], hbk_b])
                        gv = xG[:, H + n * N:H + (n + 1) * N]
                        gcs = slice(g * SEG + n * N, g * SEG + (n + 1) * N)
                        op(V, lambda e: e.tensor_tensor(out=t1[:, cs], in0=t1[:, cs], in1=gv, op=ALU.mult), R=[t1S[n], bG], W=[t1S[n]])
                        op(V, lambda e: e.tensor_tensor(out=t2[:, cs], in0=hf[:, gcs], in1=hbk[:, cs], op=ALU.add), R=[hfS[g][n], hbS[n]], W=[t2S[n], t2_b])
                        op(G, lambda e: e.tensor_tensor(out=ob[k][:, cs], in0=t1[:, cs], in1=t2[:, cs], op=ALU.mult), R=[t1S[n], t2S[n]],
                           W=[obS[k][n], ob_b[k]])
                if d == 1:
                    dma(s_mix[NCV + h, :, g * SEG:(g + 1) * SEG], ob[u.k][:], R=[ob_b[u.k]] + obS[u.k], ds=ob_ds[u.k])

        units = []
        for h in range(NH):
            for d in range(2):
                gs = [0, 1, 2] if d == 0 else [2, 1, 0]
                for gi, g in enumerate(gs):
                    units.append(Unit(h, d, g, gi == 0))
        units[0].p1a()
        for ui, u in enumerate(units):
            u.p1b()
            if ui + 1 < len(units):
                units[ui + 1].p1a()
            u.p23()
            u.p4()

        join = xcS + xcbS + t1S + t2S + hbS
        for bb in (xc_b, xcb_b, t1_b, t2_b, hbk_b):
            for sbuf_ in join:
                if sbuf_.w is not None:
                    bb.r.append(sbuf_.w)
                bb.r.extend(sbuf_.r)

        for q in range(4):
            cP = 3 * NCV + 2 * NH + q
            dma(wa[:], plw[l, q], W=[w_b], ds=w_ds)
            op(V, lambda e: e.tensor_copy(out=wab[:], in_=wa[:]), R=[w_b], W=[w_b])
            for g in range(3):
                xU, bU = load_seg(cP, g)
                dma(icv[:], invc_d[q, g * SEG:(g + 1) * SEG].partition_broadcast(128), W=[icv_b], ds=icv_ds)
                src, srcb = xU, bU
                width = WB
                dsts = [(t1, t1_b), (t2, t2_b)]
                for lev in range(q + 1):
                    sh = 1 << lev
                    dst, dstb = dsts[lev % 2]
                    nw = width - sh
                    E_ = G if lev % 2 == 0 else V
                    op(E_, lambda e: e.tensor_tensor(out=dst[:, 0:nw], in0=src[:, 0:nw], in1=src[:, sh:sh + nw], op=ALU.add), R=[srcb], W=[dstb])
                    src, srcb, width = dst, dstb, nw
                w2 = (1 << (q + 1)) // 2
                op(V, lambda e: e.tensor_tensor(out=xc[:], in0=src[:, H - w2:H - w2 + SEG], in1=icv[:], op=ALU.mult), R=[srcb, icv_b], W=[xc_b])
                op(G, lambda e: e.tensor_tensor(out=xcb[:], in0=xc[:], in1=xU[:, own], op=ALU.subtract), R=[xc_b, bU], W=[xcb_b])
                k = cn["o"] % 2
                cn["o"] += 1
                for n in range(NTL):
                    b1 = bankB()
                    cs = slice(n * N, (n + 1) * N)
                    op(P, lambda e: e.matmul(psum[:, b1, 0:N], lhsT=wab[:], rhs=xcb[:, cs], start=True, stop=True), R=[w_b, xcb_b], W=[pb[b1]])
                    op(A, lambda e: e.activation(out=ob[k][:, cs], in_=psum[:, b1, 0:N], func=AF.Identity, scale=pc(f"ps{l}_{q}"), bias=cbias[:, 2:3]),
                       R=[pb[b1], par_b, cst_b], W=[ob_b[k]])
                store_out(NCV + NH + q, g, k)
        pg.barrier()
        stack_holder[0].close()
        stack_holder[0] = glob_stack

    token_passes(first=True, l_c=None, l_a=0, last=False)
    for l in range(L):
        mixer_pass(l)
        token_passes(first=False, l_c=l, l_a=(l + 1 if l + 1 < L else None), last=(l == L - 1))
    pg.barrier()
    return nc


def core_sequences(c):
    out = []
    for core in range(8):
        if core < 4:
            out.append([("p", core), ("s", core)])
        else:
            b = 4 + 3 * (core - 4)
            out.append([("p", b), ("p", b + 1), ("p", b + 2)])
    return out


def host_prepare(c, inp):
    xp = np.asarray(inp["x_prompt"], np.float32)
    xs = np.asarray(inp["x_sample"], np.float32)
    meta = np.asarray(inp["meta_tokens"], np.float32)
    par = pack_params(c, inp)
    ident = np.eye(128, dtype=np.float32)
    shared = {"par": par, "ident": ident}
    for k in ("ffn1_w_gate", "ffn1_w_up", "ffn1_w_down", "ffn2_w_gate", "ffn2_w_up", "ffn2_w_down", "w_in", "w_out",
              "lru_w_a", "lru_w_x", "pool_w"):
        shared[k] = np.ascontiguousarray(np.asarray(inp[k], np.float32))
    in_maps = []
    for core, seqs in enumerate(core_sequences(c)):
        xin = np.zeros((c.T, c.D), np.float32)
        invc = np.ones((4, c.T), np.float32)
        pos = 0
        for kind, idx in seqs:
            x = xp[idx] if kind == "p" else xs[idx]
            Lq = x.shape[0] + c.NMETA
            xin[pos:pos + c.NMETA] = meta
            xin[pos + c.NMETA:pos + Lq] = x
            t = np.arange(Lq)
            for q, w in enumerate((2, 4, 8, 16)):
                lo = np.maximum(t - w // 2, 0)
                hi = np.minimum(t + w // 2 - 1, Lq - 1)
                invc[q, pos:pos + Lq] = (1.0 / (hi - lo + 1)).astype(np.float32)
            pos += Lq
        flags = np.zeros((128, 4), np.float32)
        if core < 4:
            flags[:, 1] = 1.0
            flags[:, 2] = 0.0
        else:
            flags[:, 2] = 1.0
        m = dict(shared)
        m.update({"xin": xin, "invc": invc, "flags": flags})
        in_maps.append(m)
    return in_maps


def host_gather(c, results, n_prompt, n_sample):
    yp = np.zeros((n_prompt, c.LP, c.D), np.float32)
    ys = np.zeros((n_sample, c.LS, c.D), np.float32)
    for core, seqs in enumerate(core_sequences(c)):
        y = results[core]["yout"]
        pos = 0
        for kind, idx in seqs:
            Lx = c.LP if kind == "p" else c.LS
            blk = y[pos + c.NMETA:pos + c.NMETA + Lx]
            if kind == "p":
                yp[idx] = blk
            else:
                ys[idx] = blk
            pos += Lx + c.NMETA
    return yp, ys


_NC_CACHE = {}


def run(c, inp):
    key = id(c)
    if key not in _NC_CACHE:
        _NC_CACHE[key] = build_program(c)
    nc = _NC_CACHE[key]
    in_maps = host_prepare(c, inp)
    res = run_bass_kernel_spmd(nc, in_maps, core_ids=list(range(8)))
    return host_gather(c, res.results, inp["x_prompt"].shape[0], inp["x_sample"].shape[0])


def kernel(**inputs):
    yp, ys = run(FULL, inputs)
    return (yp, ys)
```

```python
import numpy as np
import concourse.bass as bass
import concourse.mybir as mybir
from concourse.bass_utils import run_bass_kernel_spmd

F32 = mybir.dt.float32
BF16 = mybir.dt.bfloat16
AF = mybir.ActivationFunctionType
ALU = mybir.AluOpType

H = 8
LN_EPS = 1e-5
LRU_C = 8.0


class Cfg:
    def __init__(self, D, F, FG, NCV, NH, N, TPS, NS, SEGT, DEPTH, LP, LS, NMETA=16):
        self.D, self.F, self.FG, self.NCV, self.NH = D, F, FG, NCV, NH
        self.N, self.TPS, self.NS, self.SEGT, self.DEPTH = N, TPS, NS, SEGT, DEPTH
        self.LP, self.LS, self.NMETA = LP, LS, NMETA
        self.KC = D // 128
        self.FC = F // 128
        self.NG = self.FC // FG
        self.CIN = 3 * NCV + 2 * NH + 4
        self.KM = NCV + NH + 4
        self.S = N * TPS
        self.T = self.S * NS
        self.SEG = N * SEGT
        assert self.T == 3 * self.SEG and self.FC % FG == 0
        assert self.SEG == LP + NMETA and 2 * self.SEG == LS + 2 * NMETA
        self.alpha = float((2 * DEPTH) ** 0.25)


FULL = Cfg(D=2048, F=5632, FG=11, NCV=4, NH=8, N=344, TPS=3, NS=6, SEGT=6, DEPTH=2, LP=2048, LS=4096)


def par_layout(c):
    off = {}
    pos = 0

    def add(name, w):
        nonlocal pos
        off[name] = (pos, w)
        pos += w
    add("lnin_g", c.KC); add("lnin_b", c.KC)
    for l in range(c.DEPTH):
        for k in (1, 2, 3):
            add(f"ln{k}_g{l}", c.KC); add(f"ln{k}_b{l}", c.KC)
        for j in range(c.NCV):
            add(f"cw{l}_{j}", 3); add(f"cb{l}_{j}", 1)
        for d in range(2):
            for h in range(c.NH):
                add(f"lw{l}_{d}_{h}", 4); add(f"lb{l}_{d}_{h}", 1)
                add(f"ba{l}_{d}_{h}", 1); add(f"bx{l}_{d}_{h}", 1); add(f"lam{l}_{d}_{h}", 1)
        for q in range(4):
            add(f"ps{l}_{q}", 1)
    return off, pos


def pack_params(c, inp):
    off, npar = par_layout(c)
    par = np.zeros((128, npar), np.float32)

    def put(name, arr):
        o, w = off[name]
        par[:, o:o + w] = arr

    def cols(v):
        v = np.asarray(v, np.float32)
        return v.reshape(-1, 128).T
    put("lnin_g", cols(inp["ln_in_g"])); put("lnin_b", cols(inp["ln_in_b"]))
    for l in range(c.DEPTH):
        for k in (1, 2, 3):
            put(f"ln{k}_g{l}", cols(inp[f"ln{k}_g"][l])); put(f"ln{k}_b{l}", cols(inp[f"ln{k}_b"][l]))
        for j in range(c.NCV):
            put(f"cw{l}_{j}", np.asarray(inp["conv_w"][l])[:, j * 128:(j + 1) * 128].T)
            put(f"cb{l}_{j}", np.asarray(inp["conv_b"][l])[j * 128:(j + 1) * 128, None])
        for d in range(2):
            for h in range(c.NH):
                sl = slice(h * 128, (h + 1) * 128)
                put(f"lw{l}_{d}_{h}", np.asarray(inp["lru_conv_w"][l, d])[:, sl].T)
                put(f"lb{l}_{d}_{h}", np.asarray(inp["lru_conv_b"][l, d])[sl, None])
                put(f"ba{l}_{d}_{h}", np.asarray(inp["lru_b_a"][l, d])[sl, None])
                put(f"bx{l}_{d}_{h}", np.asarray(inp["lru_b_x"][l, d])[sl, None])
                put(f"lam{l}_{d}_{h}", np.asarray(inp["lru_lambda"][l, d])[sl, None])
        for q in range(4):
            put(f"ps{l}_{q}", np.asarray(inp["pool_scale"][l])[q * 128:(q + 1) * 128, None])
    return par


class Buf:
    __slots__ = ("w", "r")

    def __init__(self):
        self.w = None
        self.r = []


class Eng:
    def __init__(self, nc, h, name, selfsync=True, has_sem=True):
        self.h = h
        self.sem = nc.semaphore(name).__enter__() if has_sem else None
        self.cnt = 0
        self.waited = {}
        self.selfsync = selfsync


class DSem:
    def __init__(self, nc, name):
        self.sem = nc.semaphore(name).__enter__()
        self.cnt = 0


class Prog:
    def __init__(self, nc):
        self.nc = nc
        self.P = Eng(nc, nc.tensor, "sP", selfsync=False)
        self.A = Eng(nc, nc.scalar, "sA")
        self.V = Eng(nc, nc.vector, "sV")
        self.G = Eng(nc, nc.gpsimd, "sG")
        self.Q = Eng(nc, nc.sync, "sQ", has_sem=False)
        self.engs = [self.P, self.A, self.V, self.G, self.Q]
        self.dsems = []
        self._nds = 0

    def dsem(self):
        self._nds += 1
        d = DSem(self.nc, f"d{self._nds}")
        self.dsems.append(d)
        return d

    def _deps(self, E, R, W):
        deps = {}
        for b in R:
            if b.w is not None:
                s, v = b.w
                if deps.get(s, 0) < v:
                    deps[s] = v
        for b in W:
            if b.w is not None:
                s, v = b.w
                if deps.get(s, 0) < v:
                    deps[s] = v
            for (s, v) in b.r:
                if deps.get(s, 0) < v:
                    deps[s] = v
        for s, v in deps.items():
            if s is E.sem and not E.selfsync:
                continue
            if E.waited.get(s, 0) < v:
                E.h.wait_ge(s, v)
                E.waited[s] = v

    def _record(self, tk, R, W):
        for b in R:
            b.r.append(tk)
            if len(b.r) > 64:
                m = {}
                for (s, v) in b.r:
                    if m.get(s, 0) < v:
                        m[s] = v
                b.r = list(m.items())
        for b in W:
            b.w = tk
            b.r = []

    def op(self, E, fn, R=(), W=(), inc=True):
        self._deps(E, R, W)
        ins = fn(E.h)
        if inc:
            E.cnt += 1
            ins.then_inc(E.sem, 1)
            tk = (E.sem, E.cnt)
        else:
            tk = (E.sem, E.cnt + 1)
        self._record(tk, R, W)
        return ins

    def dma(self, out, in_, R=(), W=(), ds=None, E=None, hold=None):
        E = E or self.Q
        self._deps(E, R, W)
        E.h.dma_start(out=out, in_=in_).then_inc(ds.sem, 16)
        ds.cnt += 16
        if hold is not None:
            hold.append((R, W))
        else:
            self._record((ds.sem, ds.cnt), R, W)

    def flush(self, hold, ds):
        for (R, W) in hold:
            self._record((ds.sem, ds.cnt), R, W)
        del hold[:]

    def barrier(self):
        cur = [(e.sem, e.cnt) for e in self.engs if e.sem is not None] + [(d.sem, d.cnt) for d in self.dsems]
        for E in self.engs:
            for s, v in cur:
                if v > 0 and s is not E.sem and E.waited.get(s, 0) < v:
                    E.h.wait_ge(s, v)
                    E.waited[s] = v


class Ring:
    def __init__(self, pg, tiles, loader, items):
        self.pg, self.tiles, self.loader, self.items = pg, tiles, loader, items
        self.R = len(tiles)
        self.bufs = [Buf() for _ in tiles]
        self.ds = [pg.dsem() for _ in tiles]
        self.next_load = 0
        self.next_use = 0

    def _load(self):
        i = self.next_load
        if i >= len(self.items):
            return
        k = i % self.R
        self.loader(self.tiles[k], self.items[i], self.bufs[k], self.ds[k])
        self.next_load += 1

    def prime(self):
        while self.next_load < min(self.R, len(self.items)) and self.next_load - self.next_use < self.R:
            self._load()

    def get(self, item):
        i = self.next_use
        assert self.items[i] == item, (self.items[i], item)
        k = i % self.R
        return self.tiles[k], self.bufs[k]

    def done(self):
        self.next_use += 1
        while self.next_load < len(self.items) and self.next_load - self.next_use < self.R:
            self._load()


def build_program(c, debug=False):
    nc = bass.Bass("TRN2", target_bir_lowering=False)
    KC, FC, FG, NG, N, S, T, SEG, TPS, NS = c.KC, c.FC, c.FG, c.NG, c.N, c.S, c.T, c.SEG, c.TPS, c.NS
    D, F, CIN, KM, NCV, NH, L = c.D, c.F, c.CIN, c.KM, c.NCV, c.NH, c.DEPTH
    DIN = CIN * 128
    DMX = KM * 128
    WB = SEG + 2 * H
    off, NPAR = par_layout(c)

    def din(name, shape, dt=F32):
        return nc.dram_tensor(name, shape, dt, kind="ExternalInput").ap()
    xin = din("xin", [T, D])
    par_d = din("par", [128, NPAR])
    flags_d = din("flags", [128, 4])
    invc_d = din("invc", [4, T])
    ident_d = din("ident", [128, 128])
    w_g = [din("ffn1_w_gate", [L, D, F]), din("ffn2_w_gate", [L, D, F])]
    w_u = [din("ffn1_w_up", [L, D, F]), din("ffn2_w_up", [L, D, F])]
    w_d = [din("ffn1_w_down", [L, F, D]), din("ffn2_w_down", [L, F, D])]
    w_in = din("w_in", [L, D, DIN])
    w_out = din("w_out", [L, DMX, D])
    lwa = din("lru_w_a", [L, 2, NH, 128, 128])
    lwx = din("lru_w_x", [L, 2, NH, 128, 128])
    plw = din("pool_w", [L, 4, 128, 128])
    yout = nc.dram_tensor("yout", [T, D], F32, kind="ExternalOutput").ap()

    def dscr(name, shape, dt):
        if debug and name in ("s_hres0", "s_proj", "s_mix"):
            return nc.dram_tensor(name, shape, dt, kind="ExternalOutput").ap()
        return nc.dram_tensor(name, shape, dt).ap()
    s_gu = [[dscr(f"s_gu{l}_{k}", [FC, 128, 2, KC * 128], BF16) for k in range(2)] for l in range(L)]
    s_d = [[dscr(f"s_d{l}_{k}", [NG, KC, 128, FG * 128], BF16) for k in range(2)] for l in range(L)]
    s_in = [dscr(f"s_in{l}", [CIN, 128, KC * 128], BF16) for l in range(L)]
    s_out = [dscr(f"s_out{l}", [KC, 128, KM * 128], BF16) for l in range(L)]
    s_hres = [dscr(f"s_hres{l}", [KC, 128, T], F32) for l in range(L)]
    s_proj = dscr("s_proj", [CIN, 128, T], F32)
    s_mix = dscr("s_mix", [KM, 128, T], BF16)

    pg = Prog(nc)
    P, A, V, G, Q = pg.P, pg.A, pg.V, pg.G, pg.Q
    op, dma = pg.op, pg.dma

    from contextlib import ExitStack
    stack_holder = [ExitStack()]

    uniq = [0]

    def sb(name, shape, dt):
        uniq[0] += 1
        return stack_holder[0].enter_context(nc.sbuf_tensor(f"{name}_{uniq[0]}", shape, dt))

    glob_stack = stack_holder[0]
    par = sb("par_sb", [128, NPAR], F32); par_b = Buf()
    flg = sb("flg", [128, 4], F32)
    ident = sb("ident_sb", [128, 128], F32)
    onesb = sb("onesb", [128, 128], BF16)
    coef = sb("coef", [128, L * 2 * NH * 2], F32); coef_b = Buf()
    ctmp = sb("ctmp", [128, L * 2 * NH], F32)
    WG = max(KC, KM) * 128
    gu_t = [sb(f"gu{i}", [128, 2, KC * 128], BF16) for i in range(2)]
    d_t = [sb(f"dd{i}", [128, FG * 128], BF16) for i in range(3)]
    io_t = [sb(f"io{i}", [128, WG], BF16) for i in range(2)]
    psum = nc.psum_tensor("psum", [128, 8, 512], F32).__enter__()
    pb = [Buf() for _ in range(8)]
    cst_b = Buf()
    ds0 = pg.dsem()
    grp0 = []
    dma(par[:], par_d, W=[par_b], ds=ds0, hold=grp0)
    dma(flg[:], flags_d, W=[cst_b], ds=ds0, hold=grp0)
    dma(ident[:], ident_d, W=[cst_b], ds=ds0, hold=grp0)
    pg.flush(grp0, ds0)
    op(G, lambda e: e.memset(onesb[:], 1.0 / D), W=[cst_b])
    cbias = sb("cbias", [128, 4], F32)
    op(G, lambda e: e.memset(cbias[:, 0:1], LN_EPS), W=[cst_b])
    op(G, lambda e: e.memset(cbias[:, 1:2], 1.0), W=[cst_b])
    op(G, lambda e: e.memset(cbias[:, 2:3], 0.0), W=[cst_b])

    def pc(name, j=0, w=1):
        o, _ = off[name]
        return par[:, o + j:o + j + w]
    par_al = sb("par_al", [128, NPAR], F32)
    paral_b = Buf()
    op(V, lambda e: e.tensor_scalar(out=par_al[:], in0=par[:], scalar1=c.alpha, scalar2=None, op0=ALU.mult), R=[par_b], W=[paral_b])

    def pca(name, j=0, w=1):
        o, _ = off[name]
        return par_al[:, o + j:o + j + w]

    for l in range(L):
        for d in range(2):
            for h in range(NH):
                i = (l * 2 + d) * NH + h
                op(A, lambda e: e.activation(out=ctmp[:, i:i + 1], in_=pc(f"lam{l}_{d}_{h}"), func=AF.Exp, scale=-1.0),
                   R=[par_b], W=[coef_b])
    op(A, lambda e: e.activation(out=ctmp[:], in_=ctmp[:], func=AF.Ln, bias=cbias[:, 1:2], scale=1.0), R=[coef_b, cst_b], W=[coef_b])
    cview = coef[:].rearrange("p (i two) -> p i two", two=2)
    op(V, lambda e: e.tensor_scalar(out=cview[:, :, 0], in0=ctmp[:], scalar1=-LRU_C, scalar2=None, op0=ALU.mult),
       R=[coef_b], W=[coef_b])
    op(V, lambda e: e.tensor_scalar(out=cview[:, :, 1], in0=ctmp[:], scalar1=-2.0 * LRU_C, scalar2=None, op0=ALU.mult),
       R=[coef_b], W=[coef_b])

    CE = 512
    cis = [sb(f"cv_in{i}", [128, CE], F32) for i in range(3)]
    NCO = 4
    cos = [sb(f"cv_o{i}", [128, CE], BF16) for i in range(NCO)]
    cib, cob = [Buf() for _ in range(3)], [Buf() for _ in range(NCO)]
    cid, cod = [pg.dsem() for _ in range(3)], [pg.dsem() for _ in range(NCO)]
    phase_order = []
    for l in range(L):
        for k in range(2):
            if k == 1:
                phase_order.append(("out", l))
            for g in range(NG):
                phase_order.append(("gu", l, k, g))
                phase_order.append(("d", l, k, g))
            if k == 0:
                phase_order.append(("in", l))
    phase_idx = {p: i for i, p in enumerate(phase_order)}
    phase_bufs = {p: [Buf() for _ in range(NCO)] for p in phase_order}
    csteps = []

    def add_steps(ph, src_fn, dst_fn, a):
        m = CE // 128
        a0 = 0
        while a0 < a:
            an = min(m, a - a0)
            csteps.append((ph, src_fn(a0, an), dst_fn(a0, an), an))
            a0 += an
    for ph in phase_order:
        if ph[0] == "gu":
            _, l, k, g = ph
            for f in range(g * FG, (g + 1) * FG):
                for wi, wsrc in enumerate((w_g, w_u)):
                    add_steps(ph, lambda a0, an, wsrc=wsrc, f=f, l=l, k=k: wsrc[k][l, a0 * 128:(a0 + an) * 128, f * 128:(f + 1) * 128]
                              .rearrange("(kc p) j -> p kc j", p=128),
                              lambda a0, an, f=f, l=l, k=k, wi=wi: s_gu[l][k][f, :, wi, a0 * 128:(a0 + an) * 128], KC)
        elif ph[0] == "d":
            _, l, k, g = ph
            for dch in range(KC):
                add_steps(ph, lambda a0, an, l=l, k=k, g=g, dch=dch: w_d[k][l, (g * FG + a0) * 128:(g * FG + a0 + an) * 128, dch * 128:(dch + 1) * 128]
                          .rearrange("(fc p) j -> p fc j", p=128),
                          lambda a0, an, l=l, k=k, g=g, dch=dch: s_d[l][k][g, dch, :, a0 * 128:(a0 + an) * 128], FG)
        elif ph[0] == "in":
            _, l = ph
            for ci in range(CIN):
                add_steps(ph, lambda a0, an, l=l, ci=ci: w_in[l, a0 * 128:(a0 + an) * 128, ci * 128:(ci + 1) * 128].rearrange("(kc p) j -> p kc j", p=128),
                          lambda a0, an, l=l, ci=ci: s_in[l][ci, :, a0 * 128:(a0 + an) * 128], KC)
        else:
            _, l = ph
            for dch in range(KC):
                add_steps(ph, lambda a0, an, l=l, dch=dch: w_out[l, a0 * 128:(a0 + an) * 128, dch * 128:(dch + 1) * 128].rearrange("(kc p) j -> p kc j", p=128),
                          lambda a0, an, l=l, dch=dch: s_out[l][dch, :, a0 * 128:(a0 + an) * 128], KM)

    class BgConv:
        LOOK = 2
        LB = 2

        def __init__(self):
            self.i = 0
            self.in_issued = 0
            self.out_issued = 0
            self.first_ffn_end = phase_idx[("d", 0, 0, NG - 1)]

        def _issue_in(self, j):
            ph, src3, dst2, an = csteps[j]
            k = j % 3
            dma(cis[k][:, 0:an * 128].rearrange("p (a b) -> p a b", a=an), src3, W=[cib[k]], ds=cid[k])

        def _issue_out(self, j):
            ph, src3, dst2, an = csteps[j]
            k = j % NCO
            dma(dst2, cos[k][:, 0:an * 128], R=[cob[k]], W=[phase_bufs[ph][k]], ds=cod[k])

        def step(self):
            i = self.i
            if i >= len(csteps):
                return False
            while self.in_issued < min(i + 1 + self.LOOK, len(csteps)):
                self._issue_in(self.in_issued)
                self.in_issued += 1
            while self.out_issued < i - self.LB + 1:
                self._issue_out(self.out_issued)
                self.out_issued += 1
            ph, src3, dst2, an = csteps[i]
            n = an * 128
            ki, ko = i % 3, i % NCO
            op(G, lambda e: e.tensor_copy(out=cos[ko][:, 0:n], in_=cis[ki][:, 0:n]), R=[cib[ki]], W=[cob[ko]])
            self.i += 1
            return True

        def flush_out(self):
            while self.out_issued < self.i:
                self._issue_out(self.out_issued)
                self.out_issued += 1

        def steps(self, n):
            for _ in range(n):
                if not self.step():
                    break
            if self.i >= len(csteps):
                self.flush_out()

        def slot(self, kind):
            if self.i >= len(csteps):
                return
            early = phase_idx[csteps[self.i][0]] <= self.first_ffn_end
            sub = (KC + 3) // 4
            if kind == "gu":
                self.steps(2 * sub if early else sub)
            elif kind == "d":
                self.steps((FG + 3) // 4 if early else max(1, (FG + 3) // 6))
            else:
                self.steps(sub if early else max(1, sub // 2))

        def ensure(self, ph):
            pi = phase_idx[ph]
            while self.i < len(csteps) and phase_idx[csteps[self.i][0]] <= pi:
                self.step()
            self.flush_out()
    bg = BgConv()

    gu_items, d_items, io_items = [], [], []

    def plan_ffn(l, k):
        for g in range(NG):
            for f in range(g * FG, (g + 1) * FG):
                gu_items.append((l, k, f))
            for dch in range(KC):
                d_items.append((l, k, g, dch))

    def plan_A(l):
        plan_ffn(l, 0)
        for ci in range(CIN):
            io_items.append(("in", l, ci))

    def plan_C(l):
        for dch in range(KC):
            io_items.append(("out", l, dch))
        plan_ffn(l, 1)
    for s in range(NS):
        plan_A(0)
    for l in range(L):
        for s in range(NS):
            plan_C(l)
            if l + 1 < L:
                plan_A(l + 1)

    def load_gu(tile, it, b, ds):
        l, k, f = it
        ph = ("gu", l, k, f // FG)
        bg.ensure(ph)
        dma(tile[:], s_gu[l][k][f], R=phase_bufs[ph], W=[b], ds=ds)

    def load_d(tile, it, b, ds):
        l, k, g, dch = it
        ph = ("d", l, k, g)
        bg.ensure(ph)
        dma(tile[:], s_d[l][k][g, dch], R=phase_bufs[ph], W=[b], ds=ds)

    def load_io(tile, it, b, ds):
        kind, l, ci = it
        ph = (kind, l)
        bg.ensure(ph)
        if kind == "in":
            dma(tile[:, 0:KC * 128], s_in[l][ci], R=phase_bufs[ph], W=[b], ds=ds)
        else:
            dma(tile[:, 0:KM * 128], s_out[l][ci], R=phase_bufs[ph], W=[b], ds=ds)
    r_gu = Ring(pg, gu_t, load_gu, gu_items)
    r_d = Ring(pg, d_t, load_d, d_items)
    r_io = Ring(pg, io_t, load_io, io_items)
    r_gu.prime(); r_d.prime()

    pA_rot = [0]
    pB_rot = [0]

    def bankB():
        k = 4 + (pB_rot[0] % 2)
        pB_rot[0] += 1
        return k

    def token_passes(first, l_c, l_a, last):
        stack_holder[0] = ExitStack()
        res = sb("res", [128, KC, S], F32)
        hb = sb("hb", [128, max(KC, KM), S], BF16)
        hT = sb("hT", [128, FG, S], BF16)
        xt = sb("xt", [128, D], F32)
        ybt = [sb(f"ybt{i}", [128, min(4, KC), N], BF16) for i in range(2)]
        yst = [sb(f"yst{i}", [128, min(4, KC), N], BF16) for i in range(2)]
        means = [sb(f"mean{i}", [128, N], F32) for i in range(TPS)]
        rstds = [sb(f"rstd{i}", [128, N], F32) for i in range(TPS)]
        mean_bs = [Buf() for _ in range(TPS)]; rstd_bs = [Buf() for _ in range(TPS)]
        sg = [sb(f"sg{i}", [128, N], F32) for i in range(2)]
        stg = [sb(f"stg{i}", [128, N], F32) for i in range(3)]
        bst = sb("bst", [128, 4 * 6], F32); bag = sb("bag", [128, 2], F32); brs = sb("brs", [128, 1], F32)
        res_b = [[Buf() for _ in range(TPS)] for _ in range(KC)]
        hb_b = [[Buf() for _ in range(TPS)] for _ in range(max(KC, KM))]
        hT_b = [[Buf() for _ in range(TPS)] for _ in range(FG)]
        xt_b = Buf(); ybt_b = [Buf(), Buf()]; yst_b = [Buf(), Buf()]
        sg_b = [Buf(), Buf()]; stg_b = [Buf() for _ in range(3)]
        stg_ds = [pg.dsem() for _ in range(3)]
        bn_b = Buf()
        xt_ds = pg.dsem(); act_ds = pg.dsem(); act2_ds = pg.dsem(); hres_ds = pg.dsem(); out_ds = pg.dsem()
        cnts = {"ln": 0, "stg": 0}

        def ts(n):
            return slice(n * N, (n + 1) * N)

        KG = min(4, KC)

        def layer_norm(gname, bname, scaled=True, need_hb=True):
            pcs = pca if scaled else pc
            assert TPS <= 3
            for n in range(TPS):
                pm, pe2 = 2 * n, 2 * n + 1
                for gk in range(KC // KG):
                    ks = slice(gk * KG, (gk + 1) * KG)
                    rb = [res_b[kc][n] for kc in range(gk * KG, (gk + 1) * KG)]
                    j = cnts["ln"] % 2
                    cnts["ln"] += 1
                    op(V, lambda e: e.tensor_copy(out=ybt[j][:], in_=res[:, ks, ts(n)]), R=rb, W=[ybt_b[j]])
                    op(A, lambda e: e.activation(out=yst[j][:], in_=res[:, ks, ts(n)], func=AF.Square), R=rb, W=[yst_b[j]])
                    for q in range(KG):
                        kc = gk * KG + q
                        op(P, lambda e: e.matmul(psum[:, pm, 0:N], lhsT=onesb[:], rhs=ybt[j][:, q, :], start=(kc == 0), stop=(kc == KC - 1)),
                           R=[ybt_b[j], cst_b], W=[pb[pm]])
                        op(P, lambda e: e.matmul(psum[:, pe2, 0:N], lhsT=onesb[:], rhs=yst[j][:, q, :], start=(kc == 0), stop=(kc == KC - 1)),
                           R=[yst_b[j], cst_b], W=[pb[pe2]])
            for n in range(TPS):
                pm, pe2 = 2 * n, 2 * n + 1
                mean, rstd, mean_b, rstd_b = means[n], rstds[n], mean_bs[n], rstd_bs[n]
                op(V, lambda e: e.tensor_copy(out=mean[:], in_=psum[:, pm, 0:N]), R=[pb[pm]], W=[mean_b])
                op(V, lambda e: e.tensor_tensor(out=rstd[:], in0=mean[:], in1=mean[:], op=ALU.mult), R=[mean_b], W=[rstd_b])
                op(V, lambda e: e.tensor_tensor(out=rstd[:], in0=psum[:, pe2, 0:N], in1=rstd[:], op=ALU.subtract), R=[pb[pe2], rstd_b], W=[rstd_b])
                op(A, lambda e: e.activation(out=rstd[:], in_=rstd[:], func=AF.Sqrt, bias=cbias[:, 0:1], scale=1.0), R=[rstd_b, cst_b], W=[rstd_b])
                op(V, lambda e: e.reciprocal(out=rstd[:], in_=rstd[:]), R=[rstd_b], W=[rstd_b])
            for n in range(TPS):
                mean, rstd, mean_b, rstd_b = means[n], rstds[n], mean_bs[n], rstd_bs[n]
                mean_bc = mean[:].unsqueeze(1).to_broadcast([128, KG, N])
                rstd_bc = rstd[:].unsqueeze(1).to_broadcast([128, KG, N])
                for gk in range(KC // KG):
                    ks = slice(gk * KG, (gk + 1) * KG)
                    rb = [res_b[kc][n] for kc in range(gk * KG, (gk + 1) * KG)]
                    op(V, lambda e: e.tensor_tensor(out=res[:, ks, ts(n)], in0=res[:, ks, ts(n)], in1=mean_bc, op=ALU.subtract),
                       R=rb + [mean_b], W=rb)
                    op(V, lambda e: e.tensor_tensor(out=res[:, ks, ts(n)], in0=res[:, ks, ts(n)], in1=rstd_bc, op=ALU.mult),
                       R=rb + [rstd_b], W=rb)
                    for kc in range(gk * KG, (gk + 1) * KG):
                        if need_hb:
                            op(A, lambda e: e.activation(out=hb[:, kc, ts(n)], in_=res[:, kc, ts(n)], func=AF.Identity, scale=pc(gname, kc),
                                                         bias=pc(bname, kc)), R=[res_b[kc][n], par_b], W=[hb_b[kc][n]])
                        if kc % 2 == 0:
                            op(V, lambda e: e.tensor_scalar(out=res[:, kc, ts(n)], in0=res[:, kc, ts(n)], scalar1=pcs(gname, kc), scalar2=pcs(bname, kc),
                                                            op0=ALU.mult, op1=ALU.add), R=[res_b[kc][n], par_b, paral_b], W=[res_b[kc][n]])
                        else:
                            op(A, lambda e: e.activation(out=res[:, kc, ts(n)], in_=res[:, kc, ts(n)], func=AF.Identity, scale=pcs(gname, kc),
                                                         bias=pcs(bname, kc)), R=[res_b[kc][n], par_b, paral_b], W=[res_b[kc][n]])

        def ffn(l, k):
            for g in range(NG):
                for f in range(g * FG, (g + 1) * FG):
                    fl = f - g * FG
                    wt, wb = r_gu.get((l, k, f))
                    for n in range(TPS):
                        p0 = 2 * (pA_rot[0] % 2)
                        pA_rot[0] += 1
                        for kc in range(KC):
                            op(P, lambda e: e.matmul(psum[:, p0, 0:N], lhsT=wt[:, 0, kc * 128:(kc + 1) * 128], rhs=hb[:, kc, ts(n)],
                                                     start=(kc == 0), stop=(kc == KC - 1)), R=[wb, hb_b[kc][n]], W=[pb[p0]], inc=(kc == KC - 1))
                        for kc in range(KC):
                            op(P, lambda e: e.matmul(psum[:, p0 + 1, 0:N], lhsT=wt[:, 1, kc * 128:(kc + 1) * 128], rhs=hb[:, kc, ts(n)],
                                                     start=(kc == 0), stop=(kc == KC - 1)), R=[wb, hb_b[kc][n]], W=[pb[p0 + 1]], inc=(kc == KC - 1))
                        j = (p0 // 2)
                        op(A, lambda e: e.activation(out=sg[j][:], in_=psum[:, p0, 0:N], func=AF.Silu), R=[pb[p0]], W=[sg_b[j]])
                        op(V, lambda e: e.tensor_tensor(out=hT[:, fl, ts(n)], in0=psum[:, p0 + 1, 0:N], in1=sg[j][:], op=ALU.mult),
                           R=[pb[p0 + 1], sg_b[j]], W=[hT_b[fl][n]])
                    r_gu.done()
                    bg.slot("gu")
                for dch in range(KC):
                    wt, wb = r_d.get((l, k, g, dch))
                    for n in range(TPS):
                        bk = bankB()
                        for fl in range(FG):
                            op(P, lambda e: e.matmul(psum[:, bk, 0:N], lhsT=wt[:, fl * 128:(fl + 1) * 128], rhs=hT[:, fl, ts(n)],
                                                     start=(fl == 0), stop=(fl == FG - 1)), R=[wb, hT_b[fl][n]], W=[pb[bk]], inc=(fl == FG - 1))
                        op(V, lambda e: e.scalar_tensor_tensor(out=res[:, dch, ts(n)], in0=psum[:, bk, 0:N], scalar=0.5, in1=res[:, dch, ts(n)],
                                                               op0=ALU.mult, op1=ALU.add), R=[pb[bk], res_b[dch][n]], W=[res_b[dch][n]])
                    r_d.done()
                    bg.slot("d")

        for s in range(NS):
            t0 = s * S
            if first:
                nblk = (S + 127) // 128
                for b in range(nblk):
                    nb = min(128, S - b * 128)
                    c0 = b * 128
                    dma(xt[0:nb, :], xin[t0 + c0:t0 + c0 + nb, :], W=[xt_b], ds=xt_ds)
                    nch = (D + 511) // 512
                    for q in range(nch):
                        w = min(512, D - q * 512)
                        op(V, lambda e: e.bn_stats(out=bst[0:nb, q * 6:(q + 1) * 6], in_=xt[0:nb, q * 512:q * 512 + w]), R=[xt_b], W=[bn_b])
                    op(V, lambda e: e.bn_aggr(out=bag[0:nb, :], in_=bst[0:nb, 0:nch * 6]), R=[bn_b], W=[bn_b])
                    op(A, lambda e: e.activation(out=brs[0:nb, :], in_=bag[0:nb, 1:2], func=AF.Sqrt, bias=cbias[0:nb, 0:1], scale=1.0),
                       R=[bn_b, cst_b], W=[bn_b])
                    op(V, lambda e: e.reciprocal(out=brs[0:nb, :], in_=brs[0:nb, :]), R=[bn_b], W=[bn_b])
                    op(V, lambda e: e.tensor_scalar(out=xt[0:nb, :], in0=xt[0:nb, :], scalar1=bag[0:nb, 0:1], scalar2=brs[0:nb, 0:1],
                                                    op0=ALU.subtract, op1=ALU.mult), R=[xt_b, bn_b], W=[xt_b])
                    for kc in range(KC):
                        bk = bankB()
                        op(P, lambda e: e.transpose(psum[:, bk, 0:nb], xt[0:nb, kc * 128:(kc + 1) * 128], ident[0:nb, 0:nb]),
                           R=[xt_b, cst_b], W=[pb[bk]])
                        tb = [res_b[kc][n] for n in range(TPS) if n * N < c0 + nb and (n + 1) * N > c0]
                        op(V, lambda e: e.tensor_scalar(out=res[:, kc, c0:c0 + nb], in0=psum[:, bk, 0:nb], scalar1=pca("lnin_g", kc),
                                                        scalar2=pca("lnin_b", kc), op0=ALU.mult, op1=ALU.add), R=[pb[bk], par_b, paral_b], W=tb)
                        tbh = [hb_b[kc][n] for n in range(TPS) if n * N < c0 + nb and (n + 1) * N > c0]
                        op(A, lambda e: e.activation(out=hb[:, kc, c0:c0 + nb], in_=res[:, kc, c0:c0 + nb], func=AF.Copy, scale=1.0 / c.alpha),
                           R=tb, W=tbh)
            if l_c is not None:
                l = l_c
                grp = []
                for kc in range(KM):
                    dma(hb[:, kc, :], s_mix[kc, :, t0:t0 + S], W=hb_b[kc], ds=act_ds, hold=grp)
                pg.flush(grp, act_ds)
                for kc in range(KC):
                    dma(res[:, kc, :], s_hres[l][kc, :, t0:t0 + S], W=res_b[kc], ds=act2_ds, hold=grp)
                pg.flush(grp, act2_ds)
                for dch in range(KC):
                    wt, wb = r_io.get(("out", l, dch))
                    for n in range(TPS):
                        bk = bankB()
                        for kc in range(KM):
                            op(P, lambda e: e.matmul(psum[:, bk, 0:N], lhsT=wt[:, kc * 128:(kc + 1) * 128], rhs=hb[:, kc, ts(n)],
                                                     start=(kc == 0), stop=(kc == KM - 1)), R=[wb, hb_b[kc][n]], W=[pb[bk]], inc=(kc == KM - 1))
                        op(V, lambda e: e.tensor_tensor(out=res[:, dch, ts(n)], in0=psum[:, bk, 0:N], in1=res[:, dch, ts(n)], op=ALU.add),
                           R=[pb[bk], res_b[dch][n]], W=[res_b[dch][n]])
                    r_io.done()
                    bg.slot("io")
                layer_norm(f"ln2_g{l}", f"ln2_b{l}")
                ffn(l, 1)
                layer_norm(f"ln3_g{l}", f"ln3_b{l}", scaled=not last, need_hb=not last)
            if l_a is not None:
                l = l_a
                ffn(l, 0)
                r_io.prime()
                layer_norm(f"ln1_g{l}", f"ln1_b{l}")
                grp = []
                for kc in range(KC):
                    dma(s_hres[l][kc, :, t0:t0 + S], res[:, kc, :], R=res_b[kc], ds=hres_ds, hold=grp)
                pg.flush(grp, hres_ds)
                for ci in range(CIN):
                    wt, wb = r_io.get(("in", l, ci))
                    for n in range(TPS):
                        bk = bankB()
                        for kc in range(KC):
                            op(P, lambda e: e.matmul(psum[:, bk, 0:N], lhsT=wt[:, kc * 128:(kc + 1) * 128], rhs=hb[:, kc, ts(n)],
                                                     start=(kc == 0), stop=(kc == KC - 1)), R=[wb, hb_b[kc][n]], W=[pb[bk]], inc=(kc == KC - 1))
                        j = cnts["stg"] % 3
                        cnts["stg"] += 1
                        if j == 1:
                            op(A, lambda e: e.copy(out=stg[j][:], in_=psum[:, bk, 0:N]), R=[pb[bk]], W=[stg_b[j]])
                        else:
                            op(V, lambda e: e.tensor_copy(out=stg[j][:], in_=psum[:, bk, 0:N]), R=[pb[bk]], W=[stg_b[j]])
                        dma(s_proj[ci, :, t0 + n * N:t0 + (n + 1) * N], stg[j][:], R=[stg_b[j]], ds=stg_ds[j])
                    r_io.done()
                    bg.slot("io")
            if last:
                nblk = (S + 127) // 128
                for b in range(nblk):
                    nb = min(128, S - b * 128)
                    c0 = b * 128
                    nq = (D + 511) // 512
                    for kc in range(KC):
                        bk = (kc * 128) // 512
                        tb = [res_b[kc][n] for n in range(TPS) if n * N < c0 + nb and (n + 1) * N > c0]
                        op(P, lambda e: e.transpose(psum[0:nb, bk, (kc * 128) % 512:(kc * 128) % 512 + 128], res[:, kc, c0:c0 + nb], ident[:, :]),
                           R=tb + [cst_b], W=[pb[bk]])
                    for q in range(nq):
                        w = min(512, D - q * 512)
                        if q % 2 == 0:
                            op(V, lambda e: e.tensor_copy(out=xt[0:nb, q * 512:q * 512 + w], in_=psum[0:nb, q, 0:w]), R=[pb[q]], W=[xt_b])
                        else:
                            op(A, lambda e: e.copy(out=xt[0:nb, q * 512:q * 512 + w], in_=psum[0:nb, q, 0:w]), R=[pb[q]], W=[xt_b])
                    dma(yout[t0 + c0:t0 + c0 + nb, :], xt[0:nb, :], R=[xt_b], ds=out_ds)
        pg.barrier()
        stack_holder[0].close()
        stack_holder[0] = glob_stack

    def mixer_pass(l):
        stack_holder[0] = ExitStack()
        xr = [sb(f"xr{i}", [128, WB], F32) for i in range(3)]
        xc = sb("xc", [128, SEG], F32); xcb = sb("xcb", [128, SEG], BF16)
        rr = sb("rr", [128, SEG], F32); ii = sb("ii", [128, SEG], F32)
        aa = sb("aa", [128, SEG], F32); mm_ = sb("mm", [128, SEG], F32)
        hf = sb("hf", [128, T], F32); hbk = sb("hbk", [128, SEG], F32)
        t1 = sb("t1", [128, WB], F32); t2 = sb("t2", [128, WB], F32)
        ob = [sb(f"ob{i}", [128, SEG], BF16) for i in range(2)]
        icv = sb("icv", [128, SEG], F32)
        wa = sb("wa", [128, 128], F32); wab = sb("wab", [128, 128], BF16)
        wx = sb("wx", [128, 128], F32); wxb = sb("wxb", [128, 128], BF16)
        car = sb("car", [128, 2], F32)
        xr_b = [Buf() for _ in range(3)]; xr_ds = [pg.dsem() for _ in range(3)]
        xc_b, xcb_b, rr_b, ii_b, aa_b, mm_b, hbk_b, t1_b, t2_b, icv_b = (Buf() for _ in range(10))
        hf_b = [Buf() for _ in range(3)]
        ob_b = [Buf(), Buf()]; ob_ds = [pg.dsem(), pg.dsem()]
        w_b = Buf(); w_ds = pg.dsem(); icv_ds = pg.dsem(); car_b = Buf()
        cn = {"x": 0, "o": 0}
        NTL = SEG // N
        own = slice(H, H + SEG)

        def load_seg(ci, g):
            k = cn["x"] % 3
            cn["x"] += 1
            x, b = xr[k], xr_b[k]
            lo = g * SEG - H
            hi = (g + 1) * SEG + H
            clo, chi = max(lo, 0), min(hi, T)
            if clo > lo:
                op(G, lambda e: e.memset(x[:, 0:clo - lo], 0.0), W=[b])
            if chi < hi:
                op(G, lambda e: e.memset(x[:, WB - (hi - chi):WB], 0.0), W=[b])
            dma(x[:, clo - lo:chi - lo], s_proj[ci, :, clo:chi], W=[b], ds=xr_ds[k])
            if g > 0:
                op(G, lambda e: e.tensor_scalar(out=x[:, 0:H], in0=x[:, 0:H], scalar1=flg[:, g - 1:g], scalar2=None, op0=ALU.mult),
                   R=[b, cst_b], W=[b])
            if g < 2:
                op(G, lambda e: e.tensor_scalar(out=x[:, H + SEG:WB], in0=x[:, H + SEG:WB], scalar1=flg[:, g:g + 1], scalar2=None, op0=ALU.mult),
                   R=[b, cst_b], W=[b])
            if g == 2:
                op(G, lambda e: e.tensor_scalar(out=x[:, H + SEG - 16:H + SEG], in0=x[:, H + SEG - 16:H + SEG], scalar1=flg[:, 2:3], scalar2=None,
                                                op0=ALU.mult), R=[b, cst_b], W=[b])
            return x, b

        def store_out(kc, g, j):
            dma(s_mix[kc, :, g * SEG:(g + 1) * SEG], ob[j][:], R=[ob_b[j]], ds=ob_ds[j])

        for j in range(NCV):
            for g in range(3):
                xB, bB = load_seg(j, g)
                xC, bC = load_seg(NCV + j, g)
                xV, bV = load_seg(2 * NCV + j, g)
                op(G, lambda e: e.tensor_tensor(out=t1[:], in0=xC[:], in1=xV[:], op=ALU.mult), R=[bC, bV], W=[t1_b])
                op(A, lambda e: e.activation(out=xc[:], in_=t1[:, own], func=AF.Identity, scale=pc(f"cw{l}_{j}", 1), bias=pc(f"cb{l}_{j}")),
                   R=[t1_b, par_b], W=[xc_b])
                op(V, lambda e: e.scalar_tensor_tensor(out=xc[:], in0=t1[:, H - 1:H - 1 + SEG], scalar=pc(f"cw{l}_{j}", 0), in1=xc[:],
                                                       op0=ALU.mult, op1=ALU.add), R=[t1_b, par_b, xc_b], W=[xc_b])
                op(V, lambda e: e.scalar_tensor_tensor(out=xc[:], in0=t1[:, H + 1:H + 1 + SEG], scalar=pc(f"cw{l}_{j}", 2), in1=xc[:],
                                                       op0=ALU.mult, op1=ALU.add), R=[t1_b, par_b, xc_b], W=[xc_b])
                k = cn["o"] % 2
                cn["o"] += 1
                op(G, lambda e: e.tensor_tensor(out=ob[k][:], in0=xB[:, own], in1=xc[:], op=ALU.mult), R=[bB, xc_b], W=[ob_b[k]])
                store_out(j, g, k)

        def SL(n):
            return slice(n * N, (n + 1) * N)
        xcS = [Buf() for _ in range(NTL)]; xcbS = [Buf() for _ in range(NTL)]
        rrS = [Buf() for _ in range(NTL)]; iiS = [Buf() for _ in range(NTL)]
        aaS = [Buf() for _ in range(NTL)]; mmS = [Buf() for _ in range(NTL)]
        hbS = [Buf() for _ in range(NTL)]; t1S = [Buf() for _ in range(NTL)]; t2S = [Buf() for _ in range(NTL)]
        obS = [[Buf() for _ in range(NTL)] for _ in range(2)]
        hfS = [[Buf() for _ in range(NTL)] for _ in range(3)]
        class Unit:
            def __init__(u, h, d, g, first):
                u.h, u.d, u.g, u.first = h, d, g, first
                u.ix = (l * 2 + d) * NH + h
                u.lw = f"lw{l}_{d}_{h}"
                u.sgn = -1 if d == 0 else 1
                u.order = list(range(NTL)) if d == 0 else list(range(NTL - 1, -1, -1))

            def p1a(u):
                h, d, g = u.h, u.d, u.g
                u.xX, u.bX = load_seg(3 * NCV + h, g)
                xX, bX = u.xX, u.bX
                for n in u.order:
                    cs = SL(n)
                    op(G, lambda e: e.tensor_scalar(out=xc[:, cs], in0=xX[:, H + n * N:H + (n + 1) * N], scalar1=pc(u.lw, 3),
                                                    scalar2=pc(f"lb{l}_{d}_{h}"), op0=ALU.mult, op1=ALU.add),
                       R=[bX, par_b], W=[xcS[n], xc_b])
                    for kk in (1, 2, 3):
                        o_ = H + u.sgn * kk + n * N
                        op(V, lambda e: e.scalar_tensor_tensor(out=xc[:, cs], in0=xX[:, o_:o_ + N], scalar=pc(u.lw, 3 - kk), in1=xc[:, cs],
                                                               op0=ALU.mult, op1=ALU.add), R=[bX, par_b, xcS[n]], W=[xcS[n]])
                    op(V, lambda e: e.tensor_copy(out=xcb[:, cs], in_=xc[:, cs]), R=[xcS[n]], W=[xcbS[n], xcb_b])

            def p1b(u):
                h, d, g = u.h, u.d, u.g
                if u.first:
                    dma(wa[:], lwa[l, d, h], W=[w_b], ds=w_ds)
                    dma(wx[:], lwx[l, d, h], W=[w_b], ds=w_ds)
                    op(V, lambda e: e.tensor_copy(out=wab[:], in_=wa[:]), R=[w_b], W=[w_b])
                    op(V, lambda e: e.tensor_copy(out=wxb[:], in_=wx[:]), R=[w_b], W=[w_b])
                for n in u.order:
                    cs = SL(n)
                    b1, b2 = 4 + (n % 2), 6 + (n % 2)
                    op(P, lambda e: e.matmul(psum[:, b1, 0:N], lhsT=wab[:], rhs=xcb[:, cs], start=True, stop=True), R=[w_b, xcbS[n]], W=[pb[b1]])
                    op(P, lambda e: e.matmul(psum[:, b2, 0:N], lhsT=wxb[:], rhs=xcb[:, cs], start=True, stop=True), R=[w_b, xcbS[n]], W=[pb[b2]])
                    op(A, lambda e: e.activation(out=rr[:, cs], in_=psum[:, b1, 0:N], func=AF.Sigmoid, bias=pc(f"ba{l}_{d}_{h}"), scale=1.0),
                       R=[pb[b1], par_b], W=[rrS[n]])
                    op(A, lambda e: e.activation(out=ii[:, cs], in_=psum[:, b2, 0:N], func=AF.Sigmoid, bias=pc(f"bx{l}_{d}_{h}"), scale=1.0),
                       R=[pb[b2], par_b], W=[iiS[n]])
                    op(V, lambda e: e.tensor_tensor(out=ii[:, cs], in0=ii[:, cs], in1=xc[:, cs], op=ALU.mult), R=[iiS[n], xcS[n]], W=[iiS[n]])

            def p23(u):
                h, d, g, ix = u.h, u.d, u.g, u.ix
                for n in u.order:
                    cs = SL(n)
                    op(A, lambda e: e.activation(out=aa[:, cs], in_=rr[:, cs], func=AF.Exp, scale=coef[:, 2 * ix:2 * ix + 1]),
                       R=[rrS[n], coef_b], W=[aaS[n]])
                    op(A, lambda e: e.activation(out=mm_[:, cs], in_=rr[:, cs], func=AF.Exp, scale=coef[:, 2 * ix + 1:2 * ix + 2]),
                       R=[rrS[n], coef_b], W=[mmS[n]])
                for n in u.order:
                    cs = SL(n)
                    op(A, lambda e: e.activation(out=mm_[:, cs], in_=mm_[:, cs], func=AF.Sqrt, scale=-1.0, bias=cbias[:, 1:2]),
                       R=[mmS[n], cst_b], W=[mmS[n]])
                if d == 1:
                    u.xG, u.bG = load_seg(3 * NCV + NH + h, g)
                    xG, bG = u.xG, u.bG
                    u.k = cn["o"] % 2
                    cn["o"] += 1
                    for n in u.order:
                        cs = SL(n)
                        gv = xG[:, H + n * N:H + (n + 1) * N]
                        op(V, lambda e: e.tensor_tensor(out=t1[:, cs], in0=gv, in1=gv, op=ALU.mult), R=[bG], W=[t1S[n], t1_b])
                        op(G, lambda e: e.tensor_scalar(out=t1[:, cs], in0=t1[:, cs], scalar1=0.044715, scalar2=1.0, op0=ALU.mult, op1=ALU.add),
                           R=[t1S[n]], W=[t1S[n]])
                        op(V, lambda e: e.tensor_tensor(out=t1[:, cs], in0=t1[:, cs], in1=gv, op=ALU.mult), R=[t1S[n], bG], W=[t1S[n]])
                    for n in u.order:
                        cs = SL(n)
                        op(A, lambda e: e.activation(out=t1[:, cs], in_=t1[:, cs], func=AF.Sigmoid, scale=1.5957691216057308), R=[t1S[n]], W=[t1S[n]])

            def p4(u):
                h, d, g = u.h, u.d, u.g
                for n in u.order:
                    cs = SL(n)
                    op(G, lambda e: e.tensor_tensor(out=ii[:, cs], in0=ii[:, cs], in1=mm_[:, cs], op=ALU.mult), R=[iiS[n], mmS[n]], W=[iiS[n]])
                    if d == 0:
                        gcs = slice(g * SEG + n * N, g * SEG + (n + 1) * N)
                        if n == 0 and g == 0:
                            init = 0.0
                            Rx = []
                        elif n == 0:
                            op(V, lambda e: e.tensor_scalar(out=car[:, 0:1], in0=hf[:, g * SEG - 1:g * SEG], scalar1=flg[:, g - 1:g], scalar2=None,
                                                            op0=ALU.mult), R=[hfS[g - 1][NTL - 1], cst_b], W=[car_b])
                            init = car[:, 0:1]
                            Rx = [car_b]
                        else:
                            init = hf[:, g * SEG + n * N - 1:g * SEG + n * N]
                            Rx = [hfS[g][n - 1]]
                        op(V, lambda e: e.tensor_tensor_scan(out=hf[:, gcs], data0=aa[:, cs], data1=ii[:, cs], initial=init, op0=ALU.mult, op1=ALU.add),
                           R=[aaS[n], iiS[n]] + Rx, W=[hfS[g][n]])
                    else:
                        xG, bG, k = u.xG, u.bG, u.k
                        if g == 2 and n == (SEG - 17) // N:
                            c_ = SEG - 17
                            op(G, lambda e: e.tensor_scalar(out=aa[:, c_:c_ + 1], in0=aa[:, c_:c_ + 1], scalar1=flg[:, 2:3], scalar2=None, op0=ALU.mult),
                               R=[aaS[n], cst_b], W=[aaS[n]])
                        if n == NTL - 1 and g == 2:
                            init = 0.0
                            Rx = []
                        elif n == NTL - 1:
                            op(V, lambda e: e.tensor_scalar(out=car[:, 1:2], in0=hbk[:, 0:1], scalar1=flg[:, g:g + 1], scalar2=None, op0=ALU.mult),
                               R=[hbS[0], cst_b], W=[car_b])
                            init = car[:, 1:2]
                            Rx = [car_b]
                        else:
                            init = hbk[:, (n + 1) * N:(n + 1) * N + 1]
                            Rx = [hbS[n + 1]]
                        r0, r1 = n * N, (n + 1) * N
                        rev = slice(r1 - 1, r0 - 1 if r0 > 0 else None, -1)
                        op(V, lambda e: e.tensor_tensor_scan(out=hbk[:, rev], data0=aa[:, rev], data1=ii[:, rev], initial=init,
                                                             op0=ALU.mult, op1=ALU.add), R=[aaS[n], iiS[n]] + Rx, W=[hbS[n], hbk_b])
                        gv = xG[:, H + n * N:H + (n + 1) * N]
                        gcs = slice(g * SEG + n * N, g * SEG + (n + 1) * N)
                        op(V, lambda e: e.tensor_tensor(out=t1[:, cs], in0=t1[:, cs], in1=gv, op=ALU.mult), R=[t1S[n], bG], W=[t1S[n]])
                        op(V, lambda e: e.tensor_tensor(out=t2[:, cs], in0=hf[:, gcs], in1=hbk[:, cs], op=ALU.add), R=[hfS[g][n], hbS[n]], W=[t2S[n], t2_b])
                        op(G, lambda e: e.tensor_tensor(out=ob[k][:, cs], in0=t1[:, cs], in1=t2[:, cs], op=ALU.mult), R=[t1S[n], t2S[n]],
                           W=[obS[k][n], ob_b[k]])
                if d == 1:
                    dma(s_mix[NCV + h, :, g * SEG:(g + 1) * SEG], ob[u.k][:], R=[ob_b[u.k]] + obS[u.k], ds=ob_ds[u.k])

        units = []
        for h in range(NH):
            for d in range(2):
                gs = [0, 1, 2] if d == 0 else [2, 1, 0]
                for gi, g in enumerate(gs):
                    units.append(Unit(h, d, g, gi == 0))
        units[0].p1a()
        for ui, u in enumerate(units):
            u.p1b()
            if ui + 1 < len(units):
                units[ui + 1].p1a()
            u.p23()
            u.p4()

        join = xcS + xcbS + t1S + t2S + hbS
        for bb in (xc_b, xcb_b, t1_b, t2_b, hbk_b):
            for sbuf_ in join:
                if sbuf_.w is not None:
                    bb.r.append(sbuf_.w)
                bb.r.extend(sbuf_.r)

        for q in range(4):
            cP = 3 * NCV + 2 * NH + q
            dma(wa[:], plw[l, q], W=[w_b], ds=w_ds)
            op(V, lambda e: e.tensor_copy(out=wab[:], in_=wa[:]), R=[w_b], W=[w_b])
            for g in range(3):
                xU, bU = load_seg(cP, g)
                dma(icv[:], invc_d[q, g * SEG:(g + 1) * SEG].partition_broadcast(128), W=[icv_b], ds=icv_ds)
                src, srcb = xU, bU
                width = WB
                dsts = [(t1, t1_b), (t2, t2_b)]
                for lev in range(q + 1):
                    sh = 1 << lev
                    dst, dstb = dsts[lev % 2]
                    nw = width - sh
                    E_ = G if lev % 2 == 0 else V
                    op(E_, lambda e: e.tensor_tensor(out=dst[:, 0:nw], in0=src[:, 0:nw], in1=src[:, sh:sh + nw], op=ALU.add), R=[srcb], W=[dstb])
                    src, srcb, width = dst, dstb, nw
                w2 = (1 << (q + 1)) // 2
                op(V, lambda e: e.tensor_tensor(out=xc[:], in0=src[:, H - w2:H - w2 + SEG], in1=icv[:], op=ALU.mult), R=[srcb, icv_b], W=[xc_b])
                op(G, lambda e: e.tensor_tensor(out=xcb[:], in0=xc[:], in1=xU[:, own], op=ALU.subtract), R=[xc_b, bU], W=[xcb_b])
                k = cn["o"] % 2
                cn["o"] += 1
                for n in range(NTL):
                    b1 = bankB()
                    cs = slice(n * N, (n + 1) * N)
                    op(P, lambda e: e.matmul(psum[:, b1, 0:N], lhsT=wab[:], rhs=xcb[:, cs], start=True, stop=True), R=[w_b, xcb_b], W=[pb[b1]])
                    op(A, lambda e: e.activation(out=ob[k][:, cs], in_=psum[:, b1, 0:N], func=AF.Identity, scale=pc(f"ps{l}_{q}"), bias=cbias[:, 2:3]),
                       R=[pb[b1], par_b, cst_b], W=[ob_b[k]])
                store_out(NCV + NH + q, g, k)
        pg.barrier()
        stack_holder[0].close()
        stack_holder[0] = glob_stack

    token_passes(first=True, l_c=None, l_a=0, last=False)
    for l in range(L):
        mixer_pass(l)
        token_passes(first=False, l_c=l, l_a=(l + 1 if l + 1 < L else None), last=(l == L - 1))
    pg.barrier()
    return nc


def core_sequences(c):
    out = []
    for core in range(8):
        if core < 4:
            out.append([("p", core), ("s", core)])
        else:
            b = 4 + 3 * (core - 4)
            out.append([("p", b), ("p", b + 1), ("p", b + 2)])
    return out


def host_prepare(c, inp):
    xp = np.asarray(inp["x_prompt"], np.float32)
    xs = np.asarray(inp["x_sample"], np.float32)
    meta = np.asarray(inp["meta_tokens"], np.float32)
    par = pack_params(c, inp)
    ident = np.eye(128, dtype=np.float32)
    shared = {"par": par, "ident": ident}
    for k in ("ffn1_w_gate", "ffn1_w_up", "ffn1_w_down", "ffn2_w_gate", "ffn2_w_up", "ffn2_w_down", "w_in", "w_out",
              "lru_w_a", "lru_w_x", "pool_w"):
        shared[k] = np.ascontiguousarray(np.asarray(inp[k], np.float32))
    in_maps = []
    for core, seqs in enumerate(core_sequences(c)):
        xin = np.zeros((c.T, c.D), np.float32)
        invc = np.ones((4, c.T), np.float32)
        pos = 0
        for kind, idx in seqs:
            x = xp[idx] if kind == "p" else xs[idx]
            Lq = x.shape[0] + c.NMETA
            xin[pos:pos + c.NMETA] = meta
            xin[pos + c.NMETA:pos + Lq] = x
            t = np.arange(Lq)
            for q, w in enumerate((2, 4, 8, 16)):
                lo = np.maximum(t - w // 2, 0)
                hi = np.minimum(t + w // 2 - 1, Lq - 1)
                invc[q, pos:pos + Lq] = (1.0 / (hi - lo + 1)).astype(np.float32)
            pos += Lq
        flags = np.zeros((128, 4), np.float32)
        if core < 4:
            flags[:, 1] = 1.0
            flags[:, 2] = 0.0
        else:
            flags[:, 2] = 1.0
        m = dict(shared)
        m.update({"xin": xin, "invc": invc, "flags": flags})
        in_maps.append(m)
    return in_maps


def host_gather(c, results, n_prompt, n_sample):
    yp = np.zeros((n_prompt, c.LP, c.D), np.float32)
    ys = np.zeros((n_sample, c.LS, c.D), np.float32)
    for core, seqs in enumerate(core_sequences(c)):
        y = results[core]["yout"]
        pos = 0
        for kind, idx in seqs:
            Lx = c.LP if kind == "p" else c.LS
            blk = y[pos + c.NMETA:pos + c.NMETA + Lx]
            if kind == "p":
                yp[idx] = blk
            else:
                ys[idx] = blk
            pos += Lx + c.NMETA
    return yp, ys


_NC_CACHE = {}


def run(c, inp):
    key = id(c)
    if key not in _NC_CACHE:
        _NC_CACHE[key] = build_program(c)
    nc = _NC_CACHE[key]
    in_maps = host_prepare(c, inp)
    res = run_bass_kernel_spmd(nc, in_maps, core_ids=list(range(8)))
    return host_gather(c, res.results, inp["x_prompt"].shape[0], inp["x_sample"].shape[0])


def kernel(**inputs):
    yp, ys = run(FULL, inputs)
    return (yp, ys)
```
